# Optimizing a Trainium2 kernel written in Bass

```python
import math
import jax, jax.numpy as jnp
from jax import lax
import numpy as np

D_MODEL = 1024
BATCH = 8
SEQ = 4096
DEPTH = 4
DEC_BATCH = 8
DEC_SEQ = 64
PAST_LEN = 4096

CHUNK = 64
N_HEADS = 16
N_KV_HEADS = 2
HEAD_DIM = 64
GQA_GROUP = N_HEADS // N_KV_HEADS
QKV_DIM = (N_HEADS + 2 * N_KV_HEADS) * HEAD_DIM
WINDOW = 128
WIN_CHUNKS = WINDOW // CHUNK
ROPE_THETA = 10000.0
SSM_GROUP = 16
SSM_GROUPS = D_MODEL // SSM_GROUP
SSM_STATE = 64
D_FF = -(-8 * D_MODEL // (3 * 256)) * 256
N_ATTN_LAYERS = (DEPTH + 1) // 2
N_SSM_LAYERS = DEPTH // 2
DN_ALPHA = (2.0 * DEPTH) ** 0.25
DN_BETA = (8.0 * DEPTH) ** -0.25
LN_EPS = 1e-5
NEG_INF = -1e30

kernel_name = "hybrid_swa_s5_streaming_step"


def layer_norm(x, g, b):
    xf = x.astype(jnp.float32)
    mu = xf.mean(-1, keepdims=True)
    var = jnp.square(xf - mu).mean(-1, keepdims=True)
    y = (xf - mu) * lax.rsqrt(var + LN_EPS) * g.astype(jnp.float32) + b.astype(jnp.float32)
    return y.astype(x.dtype)


def rope(x, pos):
    half = HEAD_DIM // 2
    inv = ROPE_THETA ** (-jnp.arange(half, dtype=jnp.float32) / half)
    ang = pos.astype(jnp.float32)[:, None] * inv[None, :]
    cos = jnp.cos(ang)[None, :, None, :]
    sin = jnp.sin(ang)[None, :, None, :]
    xf = x.astype(jnp.float32)
    x1, x2 = xf[..., :half], xf[..., half:]
    return jnp.concatenate([x1 * cos - x2 * sin, x2 * cos + x1 * sin], axis=-1).astype(x.dtype)


def sink_softmax(s, sink):
    m = jnp.maximum(s.max(-1, keepdims=True), sink)
    e = jnp.exp(s - m)
    return e / (e.sum(-1, keepdims=True) + jnp.exp(sink - m))


def qkv_proj(x, w_qkv, b_qkv, pos):
    B, T, _ = x.shape
    qkv = x @ w_qkv + b_qkv
    nq = N_HEADS * HEAD_DIM
    nk = N_KV_HEADS * HEAD_DIM
    q = rope(qkv[..., :nq].reshape(B, T, N_HEADS, HEAD_DIM), pos)
    k = rope(qkv[..., nq:nq + nk].reshape(B, T, N_KV_HEADS, HEAD_DIM), pos)
    v = qkv[..., nq + nk:].reshape(B, T, N_KV_HEADS, HEAD_DIM)
    return q.reshape(B, T, N_KV_HEADS, GQA_GROUP, HEAD_DIM), k, v


def attn_prompt(x, w_qkv, b_qkv, sinks, w_o, b_o):
    B, S, _ = x.shape
    NC = S // CHUNK
    q, k, v = qkv_proj(x, w_qkv, b_qkv, jnp.arange(S))
    pad = ((0, 0), (WIN_CHUNKS * CHUNK, 0), (0, 0), (0, 0))
    kc = jnp.pad(k, pad).reshape(B, NC + WIN_CHUNKS, CHUNK, N_KV_HEADS, HEAD_DIM)
    vc = jnp.pad(v, pad).reshape(B, NC + WIN_CHUNKS, CHUNK, N_KV_HEADS, HEAD_DIM)
    kb = jnp.concatenate([kc[:, j:j + NC] for j in range(WIN_CHUNKS + 1)], axis=2)
    vb = jnp.concatenate([vc[:, j:j + NC] for j in range(WIN_CHUNKS + 1)], axis=2)
    key_pos = (jnp.arange(NC)[:, None] - WIN_CHUNKS) * CHUNK + jnp.arange((WIN_CHUNKS + 1) * CHUNK)[None, :]
    qb = q.reshape(B, NC, CHUNK, N_KV_HEADS, GQA_GROUP, HEAD_DIM)
    s = jnp.einsum('bnqkgd,bnpkd->bnkgqp', qb, kb).astype(jnp.float32) * (HEAD_DIM ** -0.5)
    s = jnp.where((key_pos >= 0)[None, :, None, None, None, :], s, NEG_INF)
    p = sink_softmax(s, sinks.astype(jnp.float32).reshape(1, 1, N_KV_HEADS, GQA_GROUP, 1, 1))
    o = jnp.einsum('bnkgqp,bnpkd->bnqkgd', p.astype(vb.dtype), vb).reshape(B, S, N_HEADS * HEAD_DIM)
    y = o @ w_o + b_o
    return y, k[:, S - WINDOW:], v[:, S - WINDOW:]


def attn_sample(x, ck, cv, w_qkv, b_qkv, sinks, w_o, b_o):
    B, T, _ = x.shape
    R = ck.shape[1]
    q, k, v = qkv_proj(x, w_qkv, b_qkv, PAST_LEN + jnp.arange(T))
    kk = jnp.concatenate([ck.astype(k.dtype), k], axis=1)
    vv = jnp.concatenate([cv.astype(v.dtype), v], axis=1)
    s = jnp.einsum('btkgd,bpkd->bkgtp', q, kk).astype(jnp.float32) * (HEAD_DIM ** -0.5)
    p = sink_softmax(s, sinks.astype(jnp.float32).reshape(1, N_KV_HEADS, GQA_GROUP, 1, 1))
    o = jnp.einsum('bkgtp,bpkd->btkgd', p.astype(vv.dtype), vv).reshape(B, T, N_HEADS * HEAD_DIM)
    y = o @ w_o + b_o
    return y, kk[:, -R:], vv[:, -R:]


def s5_discretize(log_dt, a_re, a_im, b_re, b_im, c_re, c_im):
    dt = jnp.exp(log_dt.astype(jnp.float32))[:, None]
    a = lax.complex(a_re.astype(jnp.float32), a_im.astype(jnp.float32))
    a_bar = jnp.exp(dt * a)
    bmat = lax.complex(b_re.astype(jnp.float32), b_im.astype(jnp.float32))
    b_bar = ((a_bar - 1.0) / a)[..., None] * bmat
    c = lax.complex(c_re.astype(jnp.float32), c_im.astype(jnp.float32))
    return a_bar, b_bar, c


def _lin_combine(e1, e2):
    a1, b1 = e1
    a2, b2 = e2
    return a1 * a2, a2 * b1 + b2


def s5_block_scan(u, s0, a_bar, b_bar, c):
    bu = jnp.einsum('gpc,blgc->blgp', b_bar, u.astype(jnp.complex64))
    bu = bu.at[:, 0].add(a_bar * s0)
    a = jnp.broadcast_to(a_bar, bu.shape)
    _, s = lax.associative_scan(_lin_combine, (a, bu), axis=1)
    y = jnp.einsum('gcp,blgp->blgc', c, s).real
    return y, s[:, -1]


def s5_glu(x, u, y, d, w_glu, b_glu):
    B, T, _ = x.shape
    y = y.reshape(B, T, D_MODEL) + d.astype(jnp.float32) * u
    z = jax.nn.gelu(y)
    gv = z @ w_glu.astype(jnp.float32) + b_glu.astype(jnp.float32)
    return (gv[..., :D_MODEL] * jax.nn.sigmoid(gv[..., D_MODEL:])).astype(x.dtype)


def ssm_prompt(x, w_in, b_in, log_dt, a_re, a_im, b_re, b_im, c_re, c_im, d, w_glu, b_glu):
    B, S, _ = x.shape
    NC = S // CHUNK
    a_bar, b_bar, c = s5_discretize(log_dt, a_re, a_im, b_re, b_im, c_re, c_im)
    u = (x @ w_in + b_in).astype(jnp.float32)
    uc = u.reshape(B, NC, CHUNK, SSM_GROUPS, SSM_GROUP).transpose(1, 0, 2, 3, 4)
    s0 = jnp.zeros((B, SSM_GROUPS, SSM_STATE), jnp.complex64)

    def step(s, u_chunk):
        y_chunk, s_new = s5_block_scan(u_chunk, s, a_bar, b_bar, c)
        return s_new, y_chunk

    s_last, ys = lax.scan(step, s0, uc)
    y = ys.transpose(1, 0, 2, 3, 4)
    return s5_glu(x, u, y, d, w_glu, b_glu), s_last.real, s_last.imag


def ssm_sample(x, st_re, st_im, w_in, b_in, log_dt, a_re, a_im, b_re, b_im, c_re, c_im, d, w_glu, b_glu):
    B, T, _ = x.shape
    a_bar, b_bar, c = s5_discretize(log_dt, a_re, a_im, b_re, b_im, c_re, c_im)
    u = (x @ w_in + b_in).astype(jnp.float32)
    s0 = lax.complex(st_re.astype(jnp.float32), st_im.astype(jnp.float32))
    y, s_last = s5_block_scan(u.reshape(B, T, SSM_GROUPS, SSM_GROUP), s0, a_bar, b_bar, c)
    return s5_glu(x, u, y, d, w_glu, b_glu), s_last.real, s_last.imag


def swiglu(x, w_up, w_down):
    h = x @ w_up
    return (jax.nn.silu(h[..., :D_FF]) * h[..., D_FF:]) @ w_down


def setup_inputs(seed: int = 0) -> dict:
    key = jax.random.key(seed)
    ks = jax.random.split(key, 32)
    f32 = jnp.float32

    def nrm(k, shape, scale):
        return jax.random.normal(k, shape, f32) * scale

    cache_rows = min(WINDOW, PAST_LEN)
    n_idx = jnp.arange(SSM_STATE, dtype=f32)
    return {
        "x_prompt": nrm(ks[0], (BATCH, SEQ, D_MODEL), 1.0),
        "x_sample": nrm(ks[1], (DEC_BATCH, DEC_SEQ, D_MODEL), 1.0),
        "cache_k": nrm(ks[2], (N_ATTN_LAYERS, DEC_BATCH, cache_rows, N_KV_HEADS, HEAD_DIM), 1.0),
        "cache_v": nrm(ks[3], (N_ATTN_LAYERS, DEC_BATCH, cache_rows, N_KV_HEADS, HEAD_DIM), 1.0),
        "state_ssm_re": nrm(ks[4], (N_SSM_LAYERS, DEC_BATCH, SSM_GROUPS, SSM_STATE), 0.1),
        "state_ssm_im": nrm(ks[5], (N_SSM_LAYERS, DEC_BATCH, SSM_GROUPS, SSM_STATE), 0.1),
        "attn_w_qkv": nrm(ks[6], (N_ATTN_LAYERS, D_MODEL, QKV_DIM), D_MODEL ** -0.5),
        "attn_b_qkv": nrm(ks[7], (N_ATTN_LAYERS, QKV_DIM), 0.02),
        "attn_sinks": nrm(ks[8], (N_ATTN_LAYERS, N_HEADS), 0.5),
        "attn_w_o": nrm(ks[9], (N_ATTN_LAYERS, N_HEADS * HEAD_DIM, D_MODEL), (N_HEADS * HEAD_DIM) ** -0.5 * DN_BETA),
        "attn_b_o": nrm(ks[10], (N_ATTN_LAYERS, D_MODEL), 0.02),
        "ssm_w_in": nrm(ks[11], (N_SSM_LAYERS, D_MODEL, D_MODEL), D_MODEL ** -0.5),
        "ssm_b_in": nrm(ks[12], (N_SSM_LAYERS, D_MODEL), 0.02),
        "ssm_log_dt": jax.random.uniform(ks[13], (N_SSM_LAYERS, SSM_GROUPS), f32, math.log(1e-3), math.log(1e-1)),
        "ssm_a_re": -0.5 + nrm(ks[14], (N_SSM_LAYERS, SSM_GROUPS, SSM_STATE), 0.01),
        "ssm_a_im": jnp.pi * n_idx + nrm(ks[15], (N_SSM_LAYERS, SSM_GROUPS, SSM_STATE), 0.01),
        "ssm_b_re": nrm(ks[16], (N_SSM_LAYERS, SSM_GROUPS, SSM_STATE, SSM_GROUP), (2.0 * SSM_GROUP) ** -0.5),
        "ssm_b_im": nrm(ks[17], (N_SSM_LAYERS, SSM_GROUPS, SSM_STATE, SSM_GROUP), (2.0 * SSM_GROUP) ** -0.5),
        "ssm_c_re": nrm(ks[18], (N_SSM_LAYERS, SSM_GROUPS, SSM_GROUP, SSM_STATE), 0.5),
        "ssm_c_im": nrm(ks[19], (N_SSM_LAYERS, SSM_GROUPS, SSM_GROUP, SSM_STATE), 0.5),
        "ssm_d": nrm(ks[20], (N_SSM_LAYERS, D_MODEL), 1.0),
        "ssm_w_glu": jnp.concatenate([
            nrm(ks[21], (N_SSM_LAYERS, D_MODEL, D_MODEL), D_MODEL ** -0.5 * DN_BETA),
            nrm(ks[22], (N_SSM_LAYERS, D_MODEL, D_MODEL), D_MODEL ** -0.5)], axis=-1),
        "ssm_b_glu": nrm(ks[23], (N_SSM_LAYERS, 2 * D_MODEL), 0.02),
        "ffn_w_up": nrm(ks[24], (DEPTH, D_MODEL, 2 * D_FF), D_MODEL ** -0.5),
        "ffn_w_down": nrm(ks[25], (DEPTH, D_FF, D_MODEL), D_FF ** -0.5 * DN_BETA),
        "ln_gain": 1.0 + nrm(ks[26], (DEPTH, 2, D_MODEL), 0.02),
        "ln_bias": nrm(ks[27], (DEPTH, 2, D_MODEL), 0.02),
    }


def reference(x_prompt, x_sample, cache_k, cache_v, state_ssm_re, state_ssm_im,
              attn_w_qkv, attn_b_qkv, attn_sinks, attn_w_o, attn_b_o,
              ssm_w_in, ssm_b_in, ssm_log_dt, ssm_a_re, ssm_a_im, ssm_b_re, ssm_b_im,
              ssm_c_re, ssm_c_im, ssm_d, ssm_w_glu, ssm_b_glu,
              ffn_w_up, ffn_w_down, ln_gain, ln_bias):
    xp, xs = x_prompt, x_sample
    pk, pv, pre, pim = [], [], [], []
    sk, sv, sre, sim = [], [], [], []
    for i in range(DEPTH):
        l = i // 2
        if i % 2 == 0:
            mp, k_p, v_p = attn_prompt(xp, attn_w_qkv[l], attn_b_qkv[l], attn_sinks[l], attn_w_o[l], attn_b_o[l])
            ms, k_s, v_s = attn_sample(xs, cache_k[l], cache_v[l], attn_w_qkv[l], attn_b_qkv[l],
                                       attn_sinks[l], attn_w_o[l], attn_b_o[l])
            pk.append(k_p); pv.append(v_p); sk.append(k_s); sv.append(v_s)
        else:
            ssm_args = (ssm_w_in[l], ssm_b_in[l], ssm_log_dt[l], ssm_a_re[l], ssm_a_im[l],
                        ssm_b_re[l], ssm_b_im[l], ssm_c_re[l], ssm_c_im[l], ssm_d[l],
                        ssm_w_glu[l], ssm_b_glu[l])
            mp, r_p, i_p = ssm_prompt(xp, *ssm_args)
            ms, r_s, i_s = ssm_sample(xs, state_ssm_re[l], state_ssm_im[l], *ssm_args)
            pre.append(r_p); pim.append(i_p); sre.append(r_s); sim.append(i_s)
        xp = layer_norm(DN_ALPHA * xp + mp, ln_gain[i, 0], ln_bias[i, 0])
        xs = layer_norm(DN_ALPHA * xs + ms, ln_gain[i, 0], ln_bias[i, 0])
        xp = layer_norm(DN_ALPHA * xp + swiglu(xp, ffn_w_up[i], ffn_w_down[i]), ln_gain[i, 1], ln_bias[i, 1])
        xs = layer_norm(DN_ALPHA * xs + swiglu(xs, ffn_w_up[i], ffn_w_down[i]), ln_gain[i, 1], ln_bias[i, 1])
    return (xp, xs,
            jnp.stack(pk), jnp.stack(pv), jnp.stack(pre), jnp.stack(pim),
            jnp.stack(sk), jnp.stack(sv), jnp.stack(sre), jnp.stack(sim))
```

```python
import contextlib
import math
import os
import numpy as np
import concourse.bass as bass
import concourse.mybir as mybir
from concourse.bass_utils import run_bass_kernel_spmd

F32 = mybir.dt.float32
BF16 = mybir.dt.bfloat16
I32 = mybir.dt.int32
AF = mybir.ActivationFunctionType
ALU = mybir.AluOpType

ENGS = ("pe", "act", "dve", "pool", "sp")

D = 1024
KC = 8
TT = 512
DFF = 2816
NJ = 22
ALPHA = 8.0 ** 0.25
EPS = 1e-5
TWO_PI = 2.0 * math.pi


class Op:
    __slots__ = ("eng", "emit", "deps", "token", "signal", "is_dma", "ring_wait", "nop", "tag")

    def __init__(self, eng, emit):
        self.eng = eng
        self.emit = emit
        self.deps = []
        self.token = None
        self.signal = False
        self.is_dma = False
        self.ring_wait = None
        self.nop = False


class _Rec:
    def __init__(self):
        self.calls = []

    def __getattr__(self, name):
        def f(*a, **k):
            self.calls.append((name, a, k))
            return None
        return f


class Sched:
    NRING = 6

    def __init__(self, nc):
        self.nc = nc
        self.streams = {e: [] for e in ENGS}
        self.last_w = {}
        self.readers = {}
        self.dma_count = {e: 0 for e in ENGS}
        self.dma_ring_ops = {e: [] for e in ENGS}
        self.out_dmas = []
        self.bar_lasts = []
        self.cur_tag = "init"
        self.annotate = bool(os.environ.get("KANNOT"))

    def barrier(self):
        lasts = []
        for e in ENGS:
            for o in reversed(self.streams[e]):
                if not o.is_dma and not o.nop:
                    lasts.append(o)
                    break
            lasts += self.dma_ring_ops[e][-self.NRING:]
        self.bar_lasts = lasts
        for e in ("pe", "act", "dve"):
            b = self.op(e, lambda eng: None)
            b.nop = True
            b.deps = list(lasts)

    def op(self, eng, emit, reads=(), writes=(), dma=False, out=False, arena=False):
        rec = _Rec()
        emit(rec)
        o = Op(eng, rec.calls)
        o.tag = self.cur_tag
        o.is_dma = dma
        deps = set(self.bar_lasts) if arena else set()
        for k in reads:
            w = self.last_w.get(k)
            if w is not None:
                deps.add(w)
            if isinstance(k, tuple) and k[0] == "ps":
                for r in self.readers.get(k, ()):
                    if r.eng != eng:
                        deps.add(r)
        for k in writes:
            w = self.last_w.get(k)
            if w is not None:
                deps.add(w)
            for r in self.readers.get(k, ()):
                deps.add(r)
        o.deps = list(deps)
        for k in reads:
            self.readers.setdefault(k, []).append(o)
        for k in writes:
            self.last_w[k] = o
            self.readers[k] = []
        if dma:
            i = self.dma_count[eng]
            self.dma_count[eng] += 1
            ring = self.dma_ring_ops[eng]
            if i >= self.NRING:
                o.ring_wait = ring[i - self.NRING]
            ring.append(o)
            o.token = (("dma", eng, i % self.NRING), 16 * (i // self.NRING + 1))
            if out:
                self.out_dmas.append(o)
        self.streams[eng].append(o)
        return o

    def emit_all(self):
        nc = self.nc
        fin = self.op("sp", lambda e: None)
        fin.deps = list(self.out_dmas)
        for e in ENGS:
            for o in self.streams[e]:
                for d in o.deps:
                    d.signal = True
        for e in ENGS:
            c = 0
            for o in self.streams[e]:
                if o.is_dma:
                    continue
                if o.signal:
                    c += 1
                    o.token = (("eng", e), c)
        semkeys = set()
        for e in ENGS:
            for o in self.streams[e]:
                if o.token is not None:
                    semkeys.add(o.token[0])
        semkeys = sorted(semkeys, key=str)
        with contextlib.ExitStack() as st:
            sems = {}
            for i, k in enumerate(semkeys):
                sems[k] = st.enter_context(nc.semaphore("s%d" % i))
            block = st.enter_context(nc.Block())
            handles = {"pe": "tensor", "act": "scalar", "dve": "vector", "pool": "gpsimd", "sp": "sync"}

            def make(e):
                def body(eng):
                    waited = {}
                    for o in self.streams[e]:
                        need = {}
                        dl = list(o.deps)
                        if o.ring_wait is not None:
                            dl.append(o.ring_wait)
                        for d in dl:
                            k, v = d.token
                            if need.get(k, 0) < v:
                                need[k] = v
                        for k, v in need.items():
                            if waited.get(k, 0) < v:
                                eng.wait_ge(sems[k], v)
                                waited[k] = v
                        ins = None
                        for (nm_, a_, k_) in o.emit:
                            ins = getattr(eng, nm_)(*a_, **k_)
                            if self.annotate:
                                ins.annotate(o.tag)
                        if o.is_dma:
                            ins.then_inc(sems[o.token[0]], 16)
                        elif o.signal:
                            ins.then_inc(sems[o.token[0]], 1)
                return body

            for e in ENGS:
                if self.streams[e]:
                    getattr(block, handles[e])(make(e))


import os
_SKIP = set(os.environ.get("KSKIP", "").split(","))


def build(n_ptiles=8, with_sample=True, depth=4):
    depth = int(os.environ.get("KDEPTH", depth))
    nc = bass.Bass("TRN2", target_bir_lowering=False)
    S = Sched(nc)
    NTOKP = n_ptiles * TT

    def din(name, shape, dt=F32):
        return nc.dram_tensor(name, list(shape), dt, kind="ExternalInput").ap()

    def dout(name, shape, dt=F32):
        return nc.dram_tensor(name, list(shape), dt, kind="ExternalOutput").ap()

    xp_d = din("xp", [NTOKP, D])
    xs_d = din("xs", [64, D])
    ck_d = din("ck", [2, 128, 128])
    cv_d = din("cv", [2, 128, 128])
    sre_d = din("sre", [2, 128, 32])
    sim_d = din("sim", [2, 128, 32])
    wqkv_d = din("wqkv", [2, 5, 128, 4096])
    bqk_d = din("bqk", [2, 128, 18])
    bv_d = din("bv", [2, 1, 128])
    snk_d = din("snk", [2, 1, 1024])
    wo_d = din("wo", [2, 4, 64, 4096])
    bo_d = din("bo", [2, 1, 1024])
    win_d = din("win", [2, 2, 128, 4096])
    bin_d = din("bin", [2, 128, 8])
    dd_d = din("dd", [2, 128, 8])
    wglu_d = din("wglu", [2, 4, 128, 4096])
    bglu_d = din("bglu", [2, 1, 2048])
    ldt_d = din("ldt", [2, 128, 32])
    are_d = din("are", [2, 128, 32])
    aim_d = din("aim", [2, 128, 32])
    bm_d = din("bm", [2, 8, 128, 1024])
    cm_d = din("cm", [2, 8, 128, 1024])
    wup_d = din("wup", [4, 11, 128, 4096])
    wdn_d = din("wdn", [4, 6, 128, 4096])
    lng_d = din("lng", [4, 2, 1, 2048])
    ident_d = din("ident", [128, 128])
    jgrid_d = din("jgrid", [128, 512])
    pos_d = din("pos", [1, NTOKP + 64])
    rc_d = din("ropec", [128, 2])

    yp_d = dout("yp", [NTOKP, D])
    ys_d = dout("ys", [64, D])
    nkp_d = dout("nkp", [2, 128, 128])
    nvp_d = dout("nvp", [2, 128, 128])
    nrp_d = dout("nrp", [2, 32, 128])
    nip_d = dout("nip", [2, 32, 128])
    nks_d = dout("nks", [2, 128, 128])
    nvs_d = dout("nvs", [2, 128, 128])
    nrs_d = dout("nrs", [2, 32, 128])
    nis_d = dout("nis", [2, 32, 128])
    tab_d = nc.dram_tensor("tabs", [2, 32, 128, 2560], BF16, kind="Internal").ap()

    st = contextlib.ExitStack()
    with st:
        def sb(name, shape, dt):
            return st.enter_context(nc.sbuf_tensor(name, list(shape), dt))

        xres = sb("xres", [128, 4, D], F32)
        xT = sb("xT", [128, KC, TT], BF16)
        lng = [sb("lng0", [128, 2048], F32)]
        stats = sb("stats", [128, 2, 6], F32)
        mv = sb("mv", [128, 2], F32)
        rstd = sb("rstd", [128, 1], F32)
        ident = sb("ident_sb", [128, 128], F32)
        ones_bf = sb("ones_bf", [128, 128], BF16)
        e0 = sb("e0", [128, 128], BF16)
        NSLOT = 7
        ring = [sb("ring%d" % i, [128, 4096], BF16) for i in range(NSLOT)]
        cosT = sb("cosT", [128, TT], F32)
        sinT = sb("sinT", [128, TT], F32)
        ropec = sb("ropec_sb", [128, 2], F32)
        KTb = [sb("KT%d" % i, [128, 128 + TT], BF16) for i in range(2)]
        Vc = [sb("Vc%d" % i, [64, 10, 128], BF16) for i in range(2)]
        kf32 = sb("kf32", [128, 128], F32)
        ktr = sb("ktr", [128, 128], F32)
        kfull = sb("kfull", [128, 128], F32)
        vfull = sb("vfull", [128, 128], F32)
        vf32 = vfull
        rowf = sb("rowf", [1, 1024], F32)
        c0 = [sb("c0_%d" % i, [128, 32, 2], F32) for i in range(2)]
        abl = [sb("abl%d" % i, [128, 32, 4], F32) for i in range(2)]
        abar = [sb("abar%d" % i, [128, 32, 2], F32) for i in range(2)]
        tiny = sb("tiny", [128, 8], F32)
        tinyp = [sb("tinyp%d" % i, [128, 2], F32) for i in range(2)]
        send = sb("send", [128, 128], F32)
        sendT = ktr[0:64, :]
        binc = sb("binc", [128, 8], F32)
        ddc = sb("ddc", [128, 8], F32)
        bqk = sb("bqk_sb", [128, 18], F32)
        sm = [sb("sm%d" % i, [128, 32], F32) for i in range(16)]
        smi = sb("smi", [128, 32], I32)
        NBF, NF32 = 35840, 5632
        arena_bf = sb("arena_bf", [128, NBF], BF16)
        arena_f = sb("arena_f", [128, NF32], F32)
        arena_i = sb("arena_i", [128, 512], I32)
        aoff = {"bf": 0, "f": 0}

        def areset():
            aoff["bf"] = 0
            aoff["f"] = 0

        def abf(n, parts=128, shape=None):
            o = aoff["bf"]; aoff["bf"] += n
            assert aoff["bf"] <= NBF, aoff
            v = arena_bf[0:parts, o:o + n]
            if shape:
                v = v.rearrange("p (a b) -> p a b", a=shape[0])
            return v

        def af(n, parts=128, shape=None):
            o = aoff["f"]; aoff["f"] += n
            assert aoff["f"] <= NF32, aoff
            v = arena_f[0:parts, o:o + n]
            if shape:
                v = v.rearrange("p (a b) -> p a b", a=shape[0])
            return v
        areset()
        jg = af(512)
        cw = [af(512) for i in range(10)]
        cwi = arena_i[:, 0:512]
        ctw = abf(2560)

        psb = [st.enter_context(nc.psum_tensor("ps%d" % i, [128, 512], F32)) for i in range(8)]
        pctr = [0]

        def ps_next():
            i = pctr[0] % 6
            pctr[0] += 1
            return psb[i], ("ps", i)

        rctr = [0]

        def wload(dram_ap, nparts=128, ncols=4096):
            i = rctr[0] % NSLOT
            rctr[0] += 1
            slot = ring[i]
            key = ("ring", i)
            S.op("pool", lambda e: e.dma_start(out=slot[0:nparts, 0:ncols].rearrange("p (a b) -> p a b", b=1024), in_=dram_ap.rearrange("p (a b) -> p a b", b=1024)), writes=[key], dma=True)
            return slot, key

        def dma_in(dst, src, wkeys, arena=False):
            S.op("sp", lambda e: e.dma_start(out=dst, in_=src), writes=wkeys, dma=True, arena=arena)

        def dma_out(dst, src, rkeys):
            S.op("sp", lambda e: e.dma_start(out=dst, in_=src), reads=rkeys, dma=True, out=True)

        def dve(fn, reads, writes):
            return S.op("dve", fn, reads=reads, writes=writes)

        def act(fn, reads, writes):
            return S.op("act", fn, reads=reads, writes=writes)

        def pe(fn, reads, writes):
            return S.op("pe", fn, reads=reads, writes=writes)

        dma_in(ident[:], ident_d, ["ident"])
        dma_in(jg[:], jgrid_d, ["jg"], arena=True)
        dma_in(ropec[:], rc_d, ["ropec"])
        S.op("pool", lambda e: e.memset(ones_bf[:], 1.0), writes=["ones"])
        S.op("pool", lambda e: e.memset(e0[:], 0.0), writes=["e0"])
        S.op("pool", lambda e: e.memset(e0[0:1, :], 1.0), writes=["e0"])
        S.op("pool", lambda e: e.memset(send[:], 0.0), writes=["send"])
        for l in range(2):
            S.op("pool", (lambda l: lambda e: e.memset(c0[l][:], 0.0))(l), writes=[("c0", l)])

        def range_sin(out_ap, y_ap, tmp_i, tmp_f, keys_r, key_w, ki, kf, scale=TWO_PI, shape_fix=None):
            dve(lambda e: e.tensor_copy(out=tmp_i, in_=y_ap), keys_r, [ki])
            dve(lambda e: e.tensor_copy(out=tmp_f, in_=tmp_i), [ki], [kf])
            dve(lambda e: e.tensor_tensor(out=tmp_f, in0=y_ap, in1=tmp_f, op=ALU.subtract), keys_r + [kf], [kf])
            act(lambda e: e.activation(out=out_ap, in_=tmp_f, func=AF.Sin, scale=scale), [kf], [key_w])

        n_ssm = 0 if "setup" in _SKIP else 2
        for s in range(n_ssm):
            ldt, are, aim, dt, lr, li = sm[0], sm[1], sm[2], sm[3], sm[4], sm[5]
            dma_in(ldt[:], ldt_d[s], ["sm0"])
            dma_in(are[:], are_d[s], ["sm1"])
            dma_in(aim[:], aim_d[s], ["sm2"])
            act(lambda e: e.activation(out=dt[:], in_=ldt[:], func=AF.Exp), ["sm0"], ["sm3"])
            dve(lambda e: e.tensor_tensor(out=lr[:], in0=dt[:], in1=are[:], op=ALU.mult), ["sm3", "sm1"], ["sm4"])
            dve(lambda e: e.tensor_tensor(out=li[:], in0=dt[:], in1=aim[:], op=ALU.mult), ["sm3", "sm2"], ["sm5"])
            ys, yc, sn, cs, mg = sm[6], sm[7], sm[8], sm[9], sm[10]
            dve(lambda e: e.tensor_scalar(out=ys[:], in0=li[:], scalar1=1.0 / TWO_PI, scalar2=None, op0=ALU.mult), ["sm5"], ["sm6"])
            dve(lambda e: e.tensor_scalar(out=yc[:], in0=ys[:], scalar1=0.25, scalar2=None, op0=ALU.add), ["sm6"], ["sm7"])
            range_sin(sn[:], ys[:], smi[:], sm[11][:], ["sm6"], "sm8", "smi", "sm11")
            range_sin(cs[:], yc[:], smi[:], sm[11][:], ["sm7"], "sm9", "smi", "sm11")
            act(lambda e: e.activation(out=mg[:], in_=lr[:], func=AF.Exp), ["sm4"], ["sm10"])
            abr, abi = sm[12], sm[13]
            dve(lambda e: e.tensor_tensor(out=abr[:], in0=mg[:], in1=cs[:], op=ALU.mult), ["sm10", "sm9"], ["sm12"])
            dve(lambda e: e.tensor_tensor(out=abi[:], in0=mg[:], in1=sn[:], op=ALU.mult), ["sm10", "sm8"], ["sm13"])
            nr, den, fr, fi = sm[14], sm[15], sm[6], sm[7]
            dve(lambda e: e.tensor_scalar(out=nr[:], in0=abr[:], scalar1=-1.0, scalar2=None, op0=ALU.add), ["sm12"], ["sm14"])
            dve(lambda e: e.tensor_tensor(out=den[:], in0=are[:], in1=are[:], op=ALU.mult), ["sm1"], ["sm15"])
            dve(lambda e: e.tensor_tensor(out=sm[11][:], in0=aim[:], in1=aim[:], op=ALU.mult), ["sm2"], ["sm11"])
            dve(lambda e: e.tensor_tensor(out=den[:], in0=den[:], in1=sm[11][:], op=ALU.add), ["sm15", "sm11"], ["sm15"])
            dve(lambda e: e.reciprocal(out=den[:], in_=den[:]), ["sm15"], ["sm15"])
            dve(lambda e: e.tensor_tensor(out=fr[:], in0=nr[:], in1=are[:], op=ALU.mult), ["sm14", "sm1"], ["sm6"])
            dve(lambda e: e.tensor_tensor(out=sm[11][:], in0=abi[:], in1=aim[:], op=ALU.mult), ["sm13", "sm2"], ["sm11"])
            dve(lambda e: e.tensor_tensor(out=fr[:], in0=fr[:], in1=sm[11][:], op=ALU.add), ["sm6", "sm11"], ["sm6"])
            dve(lambda e: e.tensor_tensor(out=fr[:], in0=fr[:], in1=den[:], op=ALU.mult), ["sm6", "sm15"], ["sm6"])
            dve(lambda e: e.tensor_tensor(out=fi[:], in0=abi[:], in1=are[:], op=ALU.mult), ["sm13", "sm1"], ["sm7"])
            dve(lambda e: e.tensor_tensor(out=sm[11][:], in0=nr[:], in1=aim[:], op=ALU.mult), ["sm14", "sm2"], ["sm11"])
            dve(lambda e: e.tensor_tensor(out=fi[:], in0=fi[:], in1=sm[11][:], op=ALU.subtract), ["sm7", "sm11"], ["sm7"])
            dve(lambda e: e.tensor_tensor(out=fi[:], in0=fi[:], in1=den[:], op=ALU.mult), ["sm7", "sm15"], ["sm7"])
            for bi, blv in enumerate((512.0, 64.0)):
                yb, ybc, sb_, cb_, mb_ = sm[8], sm[9], sm[10], sm[12], sm[13]
                dve((lambda blv: lambda e: e.tensor_scalar(out=yb[:], in0=li[:], scalar1=blv / TWO_PI, scalar2=None, op0=ALU.mult))(blv), ["sm5"], ["sm8"])
                dve(lambda e: e.tensor_scalar(out=ybc[:], in0=yb[:], scalar1=0.25, scalar2=None, op0=ALU.add), ["sm8"], ["sm9"])
                range_sin(sb_[:], yb[:], smi[:], sm[11][:], ["sm8"], "sm10", "smi", "sm11")
                range_sin(cb_[:], ybc[:], smi[:], sm[11][:], ["sm9"], "sm12", "smi", "sm11")
                act((lambda blv: lambda e: e.activation(out=mb_[:], in_=lr[:], func=AF.Exp, scale=blv))(blv), ["sm4"], ["sm13"])
                dve((lambda bi: lambda e: e.tensor_tensor(out=abl[s][:, :, 2 * bi], in0=mb_[:], in1=cb_[:], op=ALU.mult))(bi), ["sm13", "sm12"], [("abl", s)])
                dve((lambda bi: lambda e: e.tensor_tensor(out=abl[s][:, :, 2 * bi + 1], in0=mb_[:], in1=sb_[:], op=ALU.mult))(bi), ["sm13", "sm10"], [("abl", s)])
            nlr = sm[14]
            dve(lambda e: e.tensor_scalar(out=nlr[:], in0=lr[:], scalar1=-1.0, scalar2=None, op0=ALU.mult), ["sm4"], ["nlr", "sm14"])
            for q in range(32):
                kc, q4 = q // 4, q % 4
                yj, yjc, sj, cj, mn, mp = cw[0], cw[1], cw[2], cw[3], cw[4], cw[5]
                lic, lrc, frc, fic = li[:, q:q + 1], lr[:, q:q + 1], fr[:, q:q + 1], fi[:, q:q + 1]
                dve(lambda e: e.tensor_scalar(out=yj[:], in0=jg[:], scalar1=lic, scalar2=1.0 / TWO_PI, op0=ALU.mult, op1=ALU.mult), ["jg", "sm5"], ["cw0"])
                dve(lambda e: e.tensor_scalar(out=yjc[:], in0=yj[:], scalar1=0.25, scalar2=None, op0=ALU.add), ["cw0"], ["cw1"])
                range_sin(sj[:], yj[:], cwi[:], cw[6][:], ["cw0"], "cw2", "cwi", "cw6")
                range_sin(cj[:], yjc[:], cwi[:], cw[6][:], ["cw1"], "cw3", "cwi", "cw6")
                act(lambda e: e.activation(out=mp[:], in_=jg[:], func=AF.Exp, scale=lrc), ["jg", "sm4"], ["cw5"])
                act(lambda e: e.activation(out=mn[:], in_=jg[:], func=AF.Exp, scale=nlr[:, q:q + 1]), ["jg", "nlr"], ["cw4"])
                er, ei = cw[7], cw[8]
                dve(lambda e: e.tensor_tensor(out=er[:], in0=mn[:], in1=cj[:], op=ALU.mult), ["cw4", "cw3"], ["cw7"])
                dve(lambda e: e.scalar_tensor_tensor(out=ei[:], in0=mn[:], scalar=-1.0, in1=sj[:], op0=ALU.mult, op1=ALU.mult), ["cw4", "cw2"], ["cw8"])
                t9 = cw[9]
                dve(lambda e: e.tensor_scalar(out=t9[:], in0=ei[:], scalar1=fic, scalar2=None, op0=ALU.mult), ["cw8", "sm7"], ["cw9"])
                dve(lambda e: e.scalar_tensor_tensor(out=ctw[:, 0:512], in0=er[:], scalar=frc, in1=t9[:], op0=ALU.mult, op1=ALU.subtract), ["cw7", "cw9", "sm6"], ["ctw"])
                dve(lambda e: e.tensor_scalar(out=t9[:], in0=er[:], scalar1=fic, scalar2=None, op0=ALU.mult), ["cw7", "sm7"], ["cw9"])
                dve(lambda e: e.scalar_tensor_tensor(out=ctw[:, 512:1024], in0=ei[:], scalar=frc, in1=t9[:], op0=ALU.mult, op1=ALU.add), ["cw8", "cw9", "sm6"], ["ctw"])
                dve(lambda e: e.tensor_tensor(out=ctw[:, 1024:1536], in0=mp[:], in1=cj[:], op=ALU.mult), ["cw5", "cw3"], ["ctw"])
                dve(lambda e: e.scalar_tensor_tensor(out=ctw[:, 2048:2560], in0=mp[:], scalar=-1.0, in1=sj[:], op0=ALU.mult, op1=ALU.mult), ["cw5", "cw2"], ["ctw"])
                dve(lambda e: e.scalar_tensor_tensor(out=ctw[:, 1536:2048], in0=mp[:], scalar=-1.0, in1=cj[:], op0=ALU.mult, op1=ALU.mult), ["cw5", "cw3"], ["ctw"])
                S.op("sp", lambda e: e.dma_start(out=tab_d[s, q], in_=ctw[:]), reads=["ctw"], writes=[("tabd", s, q)], dma=True)

        def transpose_to_xT(nb, bl):
            S.cur_tag = "xT"
            ntok = nb * bl
            for kc in range(KC):
                p, pk = ps_next()

                def f(e, kc=kc, p=p):
                    r = None
                    for blk in range(nb):
                        r = e.transpose(p[:, blk * bl:(blk + 1) * bl], xres[0:bl, blk, kc * 128:(kc + 1) * 128], ident[0:bl, 0:bl])
                    return r
                pe(f, [("xr", b_) for b_ in range(4)] + ["ident"], [pk])
                act((lambda kc, p: lambda e: e.activation(out=xT[:, kc, 0:ntok], in_=p[:, 0:ntok], func=AF.Copy))(kc, p), [pk], [("xT", kc)])

        lnctr = [0]

        def layer_norm(i, sub, nb, bl):
            S.cur_tag = "ln"
            slot = lng[0]
            lk = ("lng", 0)
            dma_in(slot[:], lng_d[i, sub].partition_broadcast(128), [lk])
            for blk in range(nb):
                xk = ("xr", blk)
                for h in range(2):
                    dve((lambda blk, h: lambda e: e.bn_stats(out=stats[0:bl, h, :], in_=xres[0:bl, blk, h * 512:(h + 1) * 512]))(blk, h), [xk], ["stats"])
                dve(lambda e: e.bn_aggr(out=mv[0:bl, :], in_=stats[0:bl, :, :].rearrange("p a b -> p (a b)")), ["stats"], ["mv"])
                act(lambda e: e.activation(out=rstd[0:bl, :], in_=mv[0:bl, 1:2], func=AF.Sqrt, bias=EPS, scale=1.0), ["mv"], ["rstd"])
                dve(lambda e: e.reciprocal(out=rstd[0:bl, :], in_=rstd[0:bl, :]), ["rstd"], ["rstd"])
                dve((lambda blk: lambda e: e.tensor_scalar(out=xres[0:bl, blk, :], in0=xres[0:bl, blk, :], scalar1=mv[0:bl, 0:1], scalar2=rstd[0:bl, 0:1], op0=ALU.subtract, op1=ALU.mult))(blk), [xk, "mv", "rstd"], [xk])
                dve((lambda blk: lambda e: e.tensor_tensor(out=xres[0:bl, blk, :], in0=xres[0:bl, blk, :], in1=slot[0:bl, 0:1024], op=ALU.mult))(blk), [xk, lk], [xk])
                dve((lambda blk: lambda e: e.tensor_tensor(out=xres[0:bl, blk, :], in0=xres[0:bl, blk, :], in1=slot[0:bl, 1024:2048], op=ALU.add))(blk), [xk, lk], [xk])

        def residual_evac(p, pk, blk, bl, c0_, n):
            dve(lambda e: e.scalar_tensor_tensor(out=xres[0:bl, blk, c0_:c0_ + n], in0=xres[0:bl, blk, c0_:c0_ + n], scalar=ALPHA, in1=p[0:bl, 0:n], op0=ALU.mult, op1=ALU.add), [("xr", blk), pk], [("xr", blk)])

        def ffn(i, nb, bl):
            ntok = nb * bl
            S.cur_tag = "ffn"
            S.barrier()
            areset()
            g = abf(NJ * TT, shape=(NJ, TT))
            sil = [af(TT) for _ in range(2)]
            for pc in range(11):
                slot, sk = wload(wup_d[i, pc])
                sv = slot[:].rearrange("p (k c) -> p k c", k=KC)
                for jj in range(2):
                    j = 2 * pc + jj
                    pa, pak = ps_next()
                    pb, pbk = ps_next()

                    def f(e, sv=sv, jj=jj, pa=pa, pb=pb):
                        r = None
                        for which, p in ((0, pa), (1, pb)):
                            c = (2 * jj + which) * 128
                            for kc in range(KC):
                                r = e.matmul(p[:, 0:ntok], lhsT=sv[:, kc, c:c + 128], rhs=xT[:, kc, 0:ntok], start=(kc == 0), stop=(kc == KC - 1))
                        return r
                    pe(f, [sk] + [("xT", k) for k in range(KC)], [pak, pbk])
                    sl = sil[j % 2]
                    slk = ("sil", j % 2)
                    act((lambda pa, sl: lambda e: e.activation(out=sl[:, 0:ntok], in_=pa[:, 0:ntok], func=AF.Silu))(pa, sl), [pak], [slk])
                    dve((lambda pb, sl, j: lambda e: e.tensor_tensor(out=g[:, j, 0:ntok], in0=pb[:, 0:ntok], in1=sl[:, 0:ntok], op=ALU.mult))(pb, sl, j), [pbk, slk], [("g", j)])
            dsl = []
            for pc in range(6):
                slot, sk = wload(wdn_d[i, pc])
                dsl.append((slot[:].rearrange("p (k c) -> p k c", k=4), sk))
            for blk in range(nb):
                for half in range(2):
                    p, pk = ps_next()

                    def f(e, p=p, blk=blk, half=half):
                        r = None
                        for j in range(NJ):
                            sv, _ = dsl[j // 4]
                            r = e.matmul(p[0:bl, :], lhsT=g[:, j, blk * bl:(blk + 1) * bl], rhs=sv[:, j % 4, half * 512:(half + 1) * 512], start=(j == 0), stop=(j == NJ - 1))
                        return r
                    pe(f, [k for _, k in dsl] + [("g", j) for j in range(NJ)], [pk])
                    residual_evac(p, pk, blk, bl, half * 512, 512)

        def rope_tables(pos0, ntok):
            S.cur_tag = "rope"
            S.barrier()
            areset()
            rt = [af(TT) for _ in range(3)]
            rti = arena_i[:, 0:TT]
            dma_in(rt[0][:, 0:ntok], pos_d[:, pos0:pos0 + ntok].partition_broadcast(128), ["rt0"], arena=True)
            dve(lambda e: e.tensor_scalar(out=rt[1][:, 0:ntok], in0=rt[0][:, 0:ntok], scalar1=ropec[:, 0:1], scalar2=None, op0=ALU.mult), ["rt0", "ropec"], ["rt1"])
            dve(lambda e: e.tensor_scalar(out=rt[2][:, 0:ntok], in0=rt[1][:, 0:ntok], scalar1=0.25, scalar2=None, op0=ALU.add), ["rt1"], ["rt2"])
            dve(lambda e: e.tensor_copy(out=rti[:, 0:ntok], in_=rt[1][:, 0:ntok]), ["rt1"], ["rti"])
            dve(lambda e: e.tensor_copy(out=rt[0][:, 0:ntok], in_=rti[:, 0:ntok]), ["rti"], ["rt0"])
            dve(lambda e: e.tensor_tensor(out=rt[0][:, 0:ntok], in0=rt[1][:, 0:ntok], in1=rt[0][:, 0:ntok], op=ALU.subtract), ["rt1", "rt0"], ["rt0"])
            act(lambda e: e.activation(out=sinT[:, 0:ntok], in_=rt[0][:, 0:ntok], func=AF.Sin, scale=ropec[:, 1:2]), ["rt0", "ropec"], ["sinT"])
            dve(lambda e: e.tensor_copy(out=rti[:, 0:ntok], in_=rt[2][:, 0:ntok]), ["rt2"], ["rti"])
            dve(lambda e: e.tensor_copy(out=rt[0][:, 0:ntok], in_=rti[:, 0:ntok]), ["rti"], ["rt0"])
            dve(lambda e: e.tensor_tensor(out=rt[0][:, 0:ntok], in0=rt[2][:, 0:ntok], in1=rt[0][:, 0:ntok], op=ALU.subtract), ["rt2", "rt0"], ["rt0"])
            act(lambda e: e.activation(out=cosT[:, 0:ntok], in_=rt[0][:, 0:ntok], func=AF.Sin, scale=TWO_PI), ["rt0"], ["cosT"])

        def attention(l, nb, bl, first_prompt, is_sample, is_last_prompt):
            ntok = nb * bl
            nch = ntok // 64
            KT = KTb[l]
            V = Vc[l]
            kk, vk = ("KT", l), ("Vc", l)
            S.cur_tag = "attn"
            S.barrier()
            areset()
            rt = [af(TT) for _ in range(3)]
            rden = [af(TT, parts=64) for _ in range(2)]
            QT = abf(8 * TT, shape=(8, TT))
            AO = abf(16 * TT, parts=64, shape=(16, TT))
            PT = [abf(TT, parts=64) for _ in range(4)]
            bvrow = abf(128)
            borow = abf(1024)
            esink = abf(1024)
            dve(lambda e: e.memset(bvrow[:, :], 0.0), [], ["bvrow"])
            dve(lambda e: e.memset(borow[:, :], 0.0), [], ["borow"])
            dve(lambda e: e.memset(esink[:, :], 0.0), [], ["esink"])
            dma_in(bqk[:], bqk_d[l], ["bqk"])
            dma_in(rowf[:, 0:128], bv_d[l], ["rowf"])
            dve(lambda e: e.tensor_copy(out=bvrow[0:1, :], in_=rowf[:, 0:128]), ["rowf"], ["bvrow"])
            dma_in(rowf[:, 0:1024], bo_d[l], ["rowf"])
            dve(lambda e: e.tensor_copy(out=borow[0:1, :], in_=rowf[:, 0:1024]), ["rowf"], ["borow"])
            dma_in(rowf[:, 0:1024], snk_d[l], ["rowf"])
            act(lambda e: e.activation(out=esink[0:1, :], in_=rowf[:, 0:1024], func=AF.Exp), ["rowf"], ["esink"])
            if is_sample and "a_cache" not in _SKIP:
                dma_in(kf32[:], ck_d[l], ["kf32"])
                p, pk = ps_next()
                pe(lambda e: e.transpose(p[:, 0:128], kf32[:], ident[:]), ["kf32", "ident"], [pk])
                act(lambda e: e.activation(out=KT[:, 0:128], in_=p[:, 0:128], func=AF.Copy), [pk], [kk])
                dve(lambda e: e.tensor_copy(out=kfull[:, 0:64], in_=p[:, 64:128]), [pk], ["kfull"])
                dma_in(vfull[0:64, :], cv_d[l, 64:128, :], ["vfull"])
                dma_in(rt[2][0:64, 0:256].rearrange("p (c f) -> p c f", c=2), cv_d[l].rearrange("(c p) f -> p c f", c=2), ["rt2"], arena=True)
                dve(lambda e: e.tensor_copy(out=V[:, 0:2, :], in_=rt[2][0:64, 0:256].rearrange("p (c f) -> p c f", c=2)), ["rt2"], [vk])

            slots = [wload(wqkv_d[l, pc]) for pc in range(5)]

            def wcols(mt):
                slot, sk = slots[mt // 4]
                return slot[:].rearrange("p (k c) -> p k c", k=KC), (mt % 4) * 128, sk
            xkeys = [("xT", k) for k in range(KC)]
            want_kf = (is_sample or is_last_prompt) and "a_kf" not in _SKIP
            for i in range(9 if "a_qk" not in _SKIP else 0):
                pa, pak = ps_next()
                pb, pbk = ps_next()

                def f(e, i=i, pa=pa, pb=pb):
                    r = None
                    for mt, p in ((2 * i, pa), (2 * i + 1, pb)):
                        sv, c, _ = wcols(mt)
                        for kc in range(KC):
                            r = e.matmul(p[:, 0:ntok], lhsT=sv[:, kc, c:c + 128], rhs=xT[:, kc, 0:ntok], start=(kc == 0), stop=(kc == KC - 1))
                    return r
                pe(f, xkeys + [wcols(2 * i)[2], wcols(2 * i + 1)[2]], [pak, pbk])
                dve((lambda i, pa: lambda e: e.scalar_tensor_tensor(out=rt[1][:, 0:ntok], in0=pa[:, 0:ntok], scalar=bqk[:, 2 * i:2 * i + 1], in1=cosT[:, 0:ntok], op0=ALU.add, op1=ALU.mult))(i, pa), [pak, "bqk", "cosT"], ["rt1"])
                dve((lambda i, pb: lambda e: e.scalar_tensor_tensor(out=rt[2][:, 0:ntok], in0=pb[:, 0:ntok], scalar=bqk[:, 2 * i + 1:2 * i + 2], in1=sinT[:, 0:ntok], op0=ALU.add, op1=ALU.mult))(i, pb), [pbk, "bqk", "sinT"], ["rt2"])
                if i < 8:
                    dve((lambda i: lambda e: e.tensor_tensor(out=QT[:, i, 0:ntok], in0=rt[1][:, 0:ntok], in1=rt[2][:, 0:ntok], op=ALU.add))(i), ["rt1", "rt2"], [("QT", i)])
                else:
                    dve(lambda e: e.tensor_tensor(out=KT[:, 128:128 + ntok], in0=rt[1][:, 0:ntok], in1=rt[2][:, 0:ntok], op=ALU.add), ["rt1", "rt2"], [kk])
                    if want_kf and is_sample:
                        dve(lambda e: e.tensor_tensor(out=kfull[:, 64:128], in0=rt[1][:, 0:64], in1=rt[2][:, 0:64], op=ALU.add), ["rt1", "rt2"], ["kfull"])
                        p, pk = ps_next()
                        pe(lambda e: e.transpose(p[:, 0:128], kfull[:, :], ident[:]), ["kfull", "ident"], [pk])
                        dve(lambda e: e.tensor_copy(out=ktr[:, :], in_=p[:, 0:128]), [pk], ["ktr"])
                        dma_out(nks_d[l], ktr[:, :], ["ktr"])
                    elif want_kf:
                        dve(lambda e: e.tensor_tensor(out=kf32[:, :], in0=rt[1][:, ntok - 128:ntok], in1=rt[2][:, ntok - 128:ntok], op=ALU.add), ["rt1", "rt2"], ["kf32"])
                        p, pk = ps_next()
                        pe(lambda e: e.transpose(p[:, 0:128], kf32[:, :], ident[:]), ["kf32", "ident"], [pk])
                        dve(lambda e: e.tensor_copy(out=ktr[:, :], in_=p[:, 0:128]), [pk], ["ktr"])
                        dma_out(nkp_d[l], ktr[:, :], ["ktr"])
            svv, cv_, vsk = wcols(18)
            for blk in range(nb if "a_v" not in _SKIP else 0):
                p, pk = ps_next()

                def f(e, p=p, blk=blk):
                    for kc in range(KC):
                        e.matmul(p[0:bl, 0:128], lhsT=xT[:, kc, blk * bl:(blk + 1) * bl], rhs=svv[:, kc, cv_:cv_ + 128], start=(kc == 0), stop=False)
                    return e.matmul(p[0:bl, 0:128], lhsT=e0[:, 0:bl], rhs=bvrow[:, :], start=False, stop=True)
                pe(f, xkeys + [vsk, "bvrow", "e0"], [pk])
                for hh in range(bl // 64):
                    act((lambda p, blk, hh: lambda e: e.activation(out=V[:, 2 + (blk * bl) // 64 + hh, :], in_=p[64 * hh:64 * hh + 64, 0:128], func=AF.Copy))(p, blk, hh), [pk], [vk])
                if want_kf and blk == nb - 1:
                    if is_sample:
                        act((lambda p: lambda e: e.activation(out=vfull[64:128, :], in_=p[0:64, 0:128], func=AF.Copy))(p), [pk], ["vfull"])
                        dma_out(nvs_d[l], vfull[:, :], ["vfull"])
                    else:
                        dve((lambda p: lambda e: e.tensor_copy(out=vf32[:, :], in_=p[:, 0:128]))(p), [pk], ["vfull"])
                        dma_out(nvp_d[l], vf32[:, :], ["vfull"])
            ptc = [0]
            for c in range(nch if "a_sc" not in _SKIP else 0):
                for kvh in range(2):
                    r0 = kvh * 64
                    kbs = [kb for kb in range(3) if not (first_prompt and c + kb < 2)]
                    pts = []
                    for kb in kbs:
                        kbuf = c + kb
                        p, pk = ps_next()
                        pe((lambda p, kbuf, c, r0: lambda e: e.matmul(p[0:64, :].rearrange("p (h q) -> p h q", h=8), lhsT=KT[r0:r0 + 64, kbuf * 64:(kbuf + 1) * 64], rhs=QT[r0:r0 + 64, :, c * 64:(c + 1) * 64], start=True, stop=True))(p, kbuf, c, r0), [kk] + [("QT", i) for i in range(8)], [pk])
                        pt = PT[ptc[0] % 4]
                        ptk = ("PT", ptc[0] % 4)
                        ptc[0] += 1
                        act((lambda p, pt: lambda e: e.activation(out=pt[:, :], in_=p[0:64, :], func=AF.Exp, scale=0.125))(p, pt), [pk], [ptk])
                        pts.append((pt, ptk, kbuf))
                    pd, pdk = ps_next()

                    def fd(e, pd=pd, pts=pts, kvh=kvh):
                        for n, (pt, _, _) in enumerate(pts):
                            e.matmul(pd[0:64, :], lhsT=ones_bf[0:64, 0:64], rhs=pt[:, :], start=(n == 0), stop=False)
                        return e.matmul(pd[0:64, :], lhsT=e0[0:64, 0:64], rhs=esink[0:64, kvh * 512:(kvh + 1) * 512], start=False, stop=True)
                    pe(fd, [k for _, k, _ in pts] + ["ones", "e0", "esink"], [pdk])
                    rd = rden[(c * 2 + kvh) % 2]
                    rdk = ("rden", (c * 2 + kvh) % 2)
                    act((lambda pd, rd: lambda e: e.activation(out=rd[:, :], in_=pd[0:64, :], func=AF.Ln))(pd, rd), [pdk], [rdk])
                    act((lambda rd: lambda e: e.activation(out=rd[:, :], in_=rd[:, :], func=AF.Exp, scale=-1.0))(rd), [rdk], [rdk])
                    po, pok = ps_next()

                    def fo(e, po=po, pts=pts, r0=r0):
                        r = None
                        for n, (pt, _, kbuf) in enumerate(pts):
                            r = e.matmul(po[0:64, :], lhsT=V[:, kbuf, r0:r0 + 64], rhs=pt[:, :], start=(n == 0), stop=(n == len(pts) - 1))
                        return r
                    pe(fo, [k for _, k, _ in pts] + [vk], [pok])
                    dve((lambda po, rd, kvh, c: lambda e: e.tensor_tensor(out=AO[:, kvh * 8:(kvh + 1) * 8, c * 64:(c + 1) * 64], in0=po[0:64, :].rearrange("p (h q) -> p h q", h=8), in1=rd[:, :].rearrange("p (h q) -> p h q", h=8), op=ALU.mult))(po, rd, kvh, c), [pok, rdk], ["AO"])
            wsl = [wload(wo_d[l, pc], nparts=64) for pc in range(4)]
            for blk in range(nb if "a_o" not in _SKIP else 0):
                for half in range(2):
                    p, pk = ps_next()

                    def f(e, p=p, blk=blk, half=half):
                        for h in range(16):
                            sv = wsl[h // 4][0][:].rearrange("p (k c) -> p k c", k=4)
                            e.matmul(p[0:bl, :], lhsT=AO[:, h, blk * bl:(blk + 1) * bl], rhs=sv[0:64, h % 4, half * 512:(half + 1) * 512], start=(h == 0), stop=False)
                        return e.matmul(p[0:bl, :], lhsT=e0[0:64, 0:bl], rhs=borow[0:64, half * 512:(half + 1) * 512], start=False, stop=True)
                    pe(f, ["AO", "borow", "e0"] + [k for _, k in wsl], [pk])
                    residual_evac(p, pk, blk, bl, half * 512, 512)
            if not is_sample:
                act(lambda e: e.activation(out=KT[:, 0:128], in_=KT[:, ntok:ntok + 128], func=AF.Copy), [kk], [kk])
                act(lambda e: e.activation(out=V[:, 0:2, :], in_=V[:, nch:nch + 2, :], func=AF.Copy), [vk], [vk])

        ctc = [0]

        def ssm(s, nb, bl, is_sample, is_last_prompt):
            ntok = nb * bl
            sbl = ntok
            nsb = 1
            bi = 0 if bl == 128 else 1
            c0s = c0[s]
            ck_ = ("c0", s)
            S.cur_tag = "ssm"
            S.barrier()
            areset()
            u_bf = abf(KC * TT, shape=(KC, TT))
            zT = abf(KC * TT, shape=(KC, TT))
            Gs2 = [[abf(2 * TT) for _ in range(2)] for _ in range(2)]
            ctab = [abf(2560) for _ in range(2)]
            cbc = [abf(2048) for _ in range(2)]
            bglurow = abf(2048)
            bub2 = [[abf(2 * TT) for _ in range(2)] for _ in range(2)]
            tq2 = [[abf(2 * TT) for _ in range(2)] for _ in range(2)]
            wbf2 = [[abf(2 * TT) for _ in range(2)] for _ in range(2)]
            dve(lambda e: e.memset(bglurow[:, :], 0.0), [], ["bglurow"])
            wsc2 = [[af(TT) for _ in range(2)] for _ in range(2)]
            ytmp = af(TT)
            sgt = af(512)
            mtmp = af(512)
            dma_in(binc[:], bin_d[s], ["binc"])
            dma_in(ddc[:], dd_d[s], ["ddc"])
            for hh_ in range(2):
                dma_in(rowf[:, 0:1024], bglu_d[s][:, hh_ * 1024:(hh_ + 1) * 1024], ["rowf"])
                dve((lambda hh_: lambda e: e.tensor_copy(out=bglurow[0:1, hh_ * 1024:(hh_ + 1) * 1024], in_=rowf[:, 0:1024]))(hh_), ["rowf"], ["bglurow"])
            if is_sample:
                dma_in(sm[0][:], sre_d[s], ["sm0"])
                dma_in(sm[1][:], sim_d[s], ["sm1"])
                ab = abar[s]
                dve(lambda e: e.tensor_tensor(out=sm[2][:], in0=sm[0][:], in1=ab[:, :, 0], op=ALU.mult), ["sm0", ("abar", s)], ["sm2"])
                dve(lambda e: e.tensor_tensor(out=sm[3][:], in0=sm[1][:], in1=ab[:, :, 1], op=ALU.mult), ["sm1", ("abar", s)], ["sm3"])
                dve(lambda e: e.tensor_tensor(out=c0s[:, :, 0], in0=sm[2][:], in1=sm[3][:], op=ALU.subtract), ["sm2", "sm3"], [ck_])
                dve(lambda e: e.tensor_tensor(out=sm[2][:], in0=sm[1][:], in1=ab[:, :, 0], op=ALU.mult), ["sm1", ("abar", s)], ["sm2"])
                dve(lambda e: e.tensor_tensor(out=sm[3][:], in0=sm[0][:], in1=ab[:, :, 1], op=ALU.mult), ["sm0", ("abar", s)], ["sm3"])
                dve(lambda e: e.tensor_tensor(out=c0s[:, :, 1], in0=sm[2][:], in1=sm[3][:], op=ALU.add), ["sm2", "sm3"], [ck_])
            xkeys = [("xT", k) for k in range(KC)]
            wsl = [wload(win_d[s, pc]) for pc in range(2)]
            for mt in range(KC):
                slot, sk = wsl[mt // 4]
                sv = slot[:].rearrange("p (k c) -> p k c", k=KC)
                c = (mt % 4) * 128
                p, pk = ps_next()

                def f(e, p=p, sv=sv, c=c):
                    r = None
                    for kc in range(KC):
                        r = e.matmul(p[:, 0:ntok], lhsT=sv[:, kc, c:c + 128], rhs=xT[:, kc, 0:ntok], start=(kc == 0), stop=(kc == KC - 1))
                    return r
                pe(f, xkeys + [sk], [pk])
                act(lambda e: e.activation(out=u_bf[:, mt, 0:ntok], in_=p[:, 0:ntok], func=AF.Identity, bias=binc[:, mt:mt + 1], scale=1.0), [pk, "binc"], [("u", mt)])
            want_state = (is_sample or is_last_prompt) and "s_state" not in _SKIP
            for kc in range(KC if "s_core" not in _SKIP else 0):
                ci = ctc[0] % 2
                ctc[0] += 1
                bc_, bk = cbc[ci], ("cbc", ci)
                S.op("pool", lambda e: e.dma_start(out=bc_[:, 0:1024], in_=bm_d[s, kc]), writes=[bk], dma=True, arena=True)
                S.op("pool", lambda e: e.dma_start(out=bc_[:, 1024:2048], in_=cm_d[s, kc]), writes=[bk], dma=True, arena=True)
                py, pyk = psb[6 + kc % 2], ("ps", 6 + kc % 2)
                for q4 in range(4):
                    q = 4 * kc + q4
                    par = q % 2
                    tab, tk = ctab[par], ("ctab", par)
                    S.op("sp", lambda e: e.dma_start(out=tab[:], in_=tab_d[s, q]), reads=[("tabd", s, q)], writes=[tk], dma=True, arena=True)
                    Gs, bub, tq, wbf, wsc = Gs2[par], bub2[par], tq2[par], wbf2[par], wsc2[par]
                    pr, prk = ps_next()
                    pi_, pik = ps_next()
                    pe(lambda e: (e.matmul(pr[:, 0:ntok], lhsT=bc_[:, q4 * 128:(q4 + 1) * 128], rhs=u_bf[:, kc, 0:ntok], start=True, stop=True), e.matmul(pi_[:, 0:ntok], lhsT=bc_[:, 512 + q4 * 128:512 + (q4 + 1) * 128], rhs=u_bf[:, kc, 0:ntok], start=True, stop=True)), [bk, ("u", kc)], [prk, pik])
                    def w2(t):
                        return t[:].rearrange("p (a b) -> p a b", a=2)[:, :, 0:ntok]
                    bubA, bubS = w2(bub[0]), w2(bub[1])
                    tqA, tqB = w2(tq[0]), w2(tq[1])
                    wbA, wbS = w2(wbf[0]), w2(wbf[1])
                    GA, GB = w2(Gs[0]), w2(Gs[1])
                    act(lambda e: e.activation(out=bubA[:, 0, :], in_=pr[:, 0:ntok], func=AF.Copy), [prk], [("bubA", par)])
                    act(lambda e: e.activation(out=bubS[:, 1, :], in_=pr[:, 0:ntok], func=AF.Copy), [prk], [("bubS", par)])
                    act(lambda e: e.activation(out=bubA[:, 1, :], in_=pi_[:, 0:ntok], func=AF.Copy), [pik], [("bubA", par)])
                    act(lambda e: e.activation(out=bubS[:, 0, :], in_=pi_[:, 0:ntok], func=AF.Copy), [pik], [("bubS", par)])

                    def tb2(t):
                        return tab[:, t * 512:t * 512 + ntok].unsqueeze(1).broadcast_to([128, 2, ntok])
                    dve(lambda e: e.tensor_tensor(out=tqA, in0=bubA, in1=tb2(0), op=ALU.mult), [("bubA", par), tk], [("tqA", par)])
                    dve(lambda e: e.tensor_tensor(out=tqB, in0=bubS, in1=tb2(1), op=ALU.mult), [("bubS", par), tk], [("tqB", par)])
                    for blk in range(nsb if "s_scan" not in _SKIP else 0):
                        sl = slice(blk * sbl, (blk + 1) * sbl)
                        dve(lambda e: e.tensor_tensor_scan(out=wsc[0][:, sl], data0=tqA[:, 0, :], data1=tqB[:, 0, :], initial=c0s[:, q, 0:1], op0=ALU.add, op1=ALU.subtract), [("tqA", par), ("tqB", par), ck_, ("c0q", s, q)], [("wsc0", par)])
                        dve(lambda e: e.tensor_tensor_scan(out=wsc[1][:, sl], data0=tqA[:, 1, :], data1=tqB[:, 1, :], initial=c0s[:, q, 1:2], op0=ALU.add, op1=ALU.add), [("tqA", par), ("tqB", par), ck_, ("c0q", s, q)], [("wsc1", par)])
                        ecol = (blk + 1) * sbl - 1
                        if want_state and blk == nsb - 1:
                            fo = 2 * 512 + sbl - 1
                            nfo = 4 * 512 + sbl - 1
                            dve(lambda e: e.tensor_copy(out=tiny[:, 4:5], in_=tab[:, fo:fo + 1]), [tk], ["tiny"])
                            dve(lambda e: e.tensor_copy(out=tiny[:, 5:6], in_=tab[:, nfo:nfo + 1]), [tk], ["tiny"])
                            dve(lambda e: e.tensor_tensor(out=tiny[:, 0:1], in0=wsc[1][:, ecol:ecol + 1], in1=tiny[:, 5:6], op=ALU.mult), [("wsc1", par), "tiny"], ["tiny"])
                            dve(lambda e: e.scalar_tensor_tensor(out=send[:, q:q + 1], in0=wsc[0][:, ecol:ecol + 1], scalar=tiny[:, 4:5], in1=tiny[:, 0:1], op0=ALU.mult, op1=ALU.add), [("wsc0", par), "tiny"], ["send"])
                            dve(lambda e: e.tensor_tensor(out=tiny[:, 1:2], in0=wsc[0][:, ecol:ecol + 1], in1=tiny[:, 5:6], op=ALU.mult), [("wsc0", par), "tiny"], ["tiny"])
                            dve(lambda e: e.scalar_tensor_tensor(out=send[:, 32 + q:33 + q], in0=wsc[1][:, ecol:ecol + 1], scalar=tiny[:, 4:5], in1=tiny[:, 1:2], op0=ALU.mult, op1=ALU.subtract), [("wsc1", par), "tiny"], ["send"])
                        if not is_sample:
                            ar = abl[s][:, q, 2 * bi:2 * bi + 1]
                            ai = abl[s][:, q, 2 * bi + 1:2 * bi + 2]
                            ceng = os.environ.get("KCARRY", "dve")
                            tp = tinyp[par]
                            tpk = ("tinyp", par)
                            cq = ("c0q", s, q)
                            S.op(ceng, lambda e: e.tensor_tensor(out=tp[:, 0:1], in0=wsc[1][:, ecol:ecol + 1], in1=ai, op=ALU.mult), reads=[("wsc1", par), ("abl", s)], writes=[tpk])
                            S.op(ceng, lambda e: e.tensor_tensor(out=tp[:, 1:2], in0=wsc[0][:, ecol:ecol + 1], in1=ai, op=ALU.mult), reads=[("wsc0", par), ("abl", s)], writes=[tpk])
                            S.op(ceng, lambda e: e.tensor_scalar(out=c0s[:, q, 0:1], in0=wsc[0][:, ecol:ecol + 1], scalar1=ar, scalar2=tp[:, 0:1], op0=ALU.mult, op1=ALU.subtract), reads=[("wsc0", par), tpk, ("abl", s)], writes=[cq])
                            S.op(ceng, lambda e: e.tensor_scalar(out=c0s[:, q, 1:2], in0=wsc[1][:, ecol:ecol + 1], scalar1=ar, scalar2=tp[:, 1:2], op0=ALU.mult, op1=ALU.add), reads=[("wsc1", par), tpk, ("abl", s)], writes=[cq])
                    act(lambda e: e.activation(out=wbA[:, 0, :], in_=wsc[0][:, 0:ntok], func=AF.Copy), [("wsc0", par)], [("wbA", par)])
                    act(lambda e: e.activation(out=wbS[:, 1, :], in_=wsc[0][:, 0:ntok], func=AF.Copy), [("wsc0", par)], [("wbS", par)])
                    act(lambda e: e.activation(out=wbA[:, 1, :], in_=wsc[1][:, 0:ntok], func=AF.Copy), [("wsc1", par)], [("wbA", par)])
                    act(lambda e: e.activation(out=wbS[:, 0, :], in_=wsc[1][:, 0:ntok], func=AF.Copy), [("wsc1", par)], [("wbS", par)])
                    frn = tab[:, 1024:2048].rearrange("p (a b) -> p a b", a=2)[:, :, 0:ntok]
                    dve(lambda e: e.tensor_tensor(out=GA, in0=wbA, in1=frn, op=ALU.mult), [("wbA", par), tk], [("GA", par)])
                    dve(lambda e: e.tensor_tensor(out=GB, in0=wbS, in1=tb2(4), op=ALU.mult), [("wbS", par), tk], [("GB", par)])

                    def fc(e):
                        cre = bc_[:, 1024 + q4 * 128:1024 + (q4 + 1) * 128]
                        cim = bc_[:, 1536 + q4 * 128:1536 + (q4 + 1) * 128]
                        e.matmul(py[:, 0:ntok], lhsT=cre, rhs=GA[:, 0, :], start=(q4 == 0), stop=False)
                        e.matmul(py[:, 0:ntok], lhsT=cre, rhs=GB[:, 0, :], start=False, stop=False)
                        e.matmul(py[:, 0:ntok], lhsT=cim, rhs=GA[:, 1, :], start=False, stop=False)
                        return e.matmul(py[:, 0:ntok], lhsT=cim, rhs=GB[:, 1, :], start=False, stop=(q4 == 3))
                    pe(fc, [bk, ("GA", par), ("GB", par)], [pyk])
                dve(lambda e: e.scalar_tensor_tensor(out=ytmp[:, 0:ntok], in0=u_bf[:, kc, 0:ntok], scalar=ddc[:, kc:kc + 1], in1=py[:, 0:ntok], op0=ALU.mult, op1=ALU.add), [pyk, ("u", kc), "ddc"], ["ytmp"])
                act(lambda e: e.activation(out=zT[:, kc, 0:ntok], in_=ytmp[:, 0:ntok], func=AF.Gelu_apprx_tanh), ["ytmp"], [("zT", kc)])
            if want_state:
                p, pk = ps_next()
                pe(lambda e: e.transpose(p[:, 0:128], send[:, :], ident[:]), ["send", "ident"], [pk])
                dve(lambda e: e.tensor_copy(out=sendT[:, :], in_=p[0:64, 0:128]), [pk], ["ktr"])
                if is_sample:
                    dma_out(nrs_d[s], sendT[0:32, :], ["ktr"])
                    dma_out(nis_d[s], sendT[32:64, :], ["ktr"])
                else:
                    dma_out(nrp_d[s], sendT[0:32, :], ["ktr"])
                    dma_out(nip_d[s], sendT[32:64, :], ["ktr"])
            S.cur_tag = "glu"
            zkeys = [("zT", k) for k in range(KC)]
            for pg in range(2 if "s_glu" not in _SKIP else 0):
                sa, sak = wload(wglu_d[s, 2 * pg])
                sb_, sbk = wload(wglu_d[s, 2 * pg + 1])
                sav = sa[:].rearrange("p (k c) -> p k c", k=KC)
                sbv = sb_[:].rearrange("p (k c) -> p k c", k=KC)
                for blk in range(nb):
                    pa, pak = ps_next()
                    pb, pbk = ps_next()

                    def f(e, pa=pa, pb=pb, blk=blk, pg=pg):
                        r = None
                        for p, sv, bo_ in ((pa, sav, 512 * pg), (pb, sbv, 1024 + 512 * pg)):
                            for kc in range(KC):
                                e.matmul(p[0:bl, :], lhsT=zT[:, kc, blk * bl:(blk + 1) * bl], rhs=sv[:, kc, :], start=(kc == 0), stop=False)
                            r = e.matmul(p[0:bl, :], lhsT=e0[:, 0:bl], rhs=bglurow[:, bo_:bo_ + 512], start=False, stop=True)
                        return r
                    pe(f, zkeys + [sak, sbk, "bglurow", "e0"], [pak, pbk])
                    act((lambda pb: lambda e: e.activation(out=sgt[0:bl, :], in_=pb[0:bl, :], func=AF.Sigmoid))(pb), [pbk], ["sgt"])
                    dve((lambda pa: lambda e: e.tensor_tensor(out=mtmp[0:bl, :], in0=pa[0:bl, :], in1=sgt[0:bl, :], op=ALU.mult))(pa), [pak, "sgt"], ["mtmp"])
                    dve((lambda blk, pg: lambda e: e.scalar_tensor_tensor(out=xres[0:bl, blk, 512 * pg:512 * pg + 512], in0=xres[0:bl, blk, 512 * pg:512 * pg + 512], scalar=ALPHA, in1=mtmp[0:bl, :], op0=ALU.mult, op1=ALU.add))(blk, pg), [("xr", blk), "mtmp"], [("xr", blk)])

        for s in range(n_ssm):
            dma_in(sm[0][:], ldt_d[s], ["sm0"])
            dma_in(sm[1][:], are_d[s], ["sm1"])
            dma_in(sm[2][:], aim_d[s], ["sm2"])
            act(lambda e: e.activation(out=sm[3][:], in_=sm[0][:], func=AF.Exp), ["sm0"], ["sm3"])
            dve(lambda e: e.tensor_tensor(out=sm[4][:], in0=sm[3][:], in1=sm[1][:], op=ALU.mult), ["sm3", "sm1"], ["sm4"])
            dve(lambda e: e.tensor_tensor(out=sm[5][:], in0=sm[3][:], in1=sm[2][:], op=ALU.mult), ["sm3", "sm2"], ["sm5"])
            dve(lambda e: e.tensor_scalar(out=sm[6][:], in0=sm[5][:], scalar1=1.0 / TWO_PI, scalar2=None, op0=ALU.mult), ["sm5"], ["sm6"])
            dve(lambda e: e.tensor_scalar(out=sm[7][:], in0=sm[6][:], scalar1=0.25, scalar2=None, op0=ALU.add), ["sm6"], ["sm7"])
            range_sin(sm[8][:], sm[6][:], smi[:], sm[11][:], ["sm6"], "sm8", "smi", "sm11")
            range_sin(sm[9][:], sm[7][:], smi[:], sm[11][:], ["sm7"], "sm9", "smi", "sm11")
            act(lambda e: e.activation(out=sm[10][:], in_=sm[4][:], func=AF.Exp), ["sm4"], ["sm10"])
            dve((lambda s: lambda e: e.tensor_tensor(out=abar[s][:, :, 0], in0=sm[10][:], in1=sm[9][:], op=ALU.mult))(s), ["sm10", "sm9"], [("abar", s)])
            dve((lambda s: lambda e: e.tensor_tensor(out=abar[s][:, :, 1], in0=sm[10][:], in1=sm[8][:], op=ALU.mult))(s), ["sm10", "sm8"], [("abar", s)])

        tiles = [(t * TT, 4, 128, False) for t in range(n_ptiles)]
        if with_sample and "sample" not in _SKIP:
            tiles.append((NTOKP, 1, 64, True))
        for ti, (pos0, nb, bl, is_sample) in enumerate(tiles):
            ntok = nb * bl
            first_prompt = (ti == 0)
            is_last_prompt = (not is_sample) and ti == n_ptiles - 1
            src = xs_d if is_sample else xp_d[pos0:pos0 + ntok, :]
            dma_in(xres[0:bl, 0:nb, :], src.rearrange("(b p) d -> p b d", p=bl), [("xr", b_) for b_ in range(4)])
            transpose_to_xT(nb, bl)
            if "rope" not in _SKIP:
                rope_tables(pos0, ntok)
            for i in range(depth):
                if i % 2 == 0:
                    if "attn" not in _SKIP:
                        attention(i // 2, nb, bl, first_prompt, is_sample, is_last_prompt)
                else:
                    if "ssm" not in _SKIP:
                        ssm(i // 2, nb, bl, is_sample, is_last_prompt)
                if "ln" not in _SKIP:
                    layer_norm(i, 0, nb, bl)
                transpose_to_xT(nb, bl)
                if "ffn" not in _SKIP:
                    ffn(i, nb, bl)
                if "ln" not in _SKIP:
                    layer_norm(i, 1, nb, bl)
                if i < depth - 1:
                    transpose_to_xT(nb, bl)
            dst = ys_d if is_sample else yp_d[pos0:pos0 + ntok, :]
            dma_out(dst.rearrange("(b p) d -> p b d", p=bl), xres[0:bl, 0:nb, :], [("xr", b_) for b_ in range(4)])
        S.emit_all()
    return nc


def _prep_shared(inp):
    f = np.float32
    sh = {}
    wq = inp["attn_w_qkv"]; bq = inp["attn_b_qkv"]
    wqkv = np.zeros((2, 5, 128, 8, 512), f)
    bqk = np.zeros((2, 128, 18), f)
    for l in range(2):
        cols = []
        for i in range(9):
            if i < 8:
                c = np.concatenate([np.arange(i * 64, i * 64 + 64), np.arange((8 + i) * 64, (8 + i) * 64 + 64)])
            else:
                c = np.arange(1024, 1152)
            cp = c.reshape(2, 2, 32)[:, ::-1, :].reshape(128)
            cols.append(c); cols.append(cp)
        cols.append(np.arange(1152, 1280))
        for mt, c in enumerate(cols):
            w = wq[l][:, c].reshape(8, 128, 128).transpose(1, 0, 2)
            wqkv[l, mt // 4, :, :, (mt % 4) * 128:(mt % 4) * 128 + 128] = w
            if mt < 18:
                bqk[l, :, mt] = bq[l][c]
    sh["wqkv"] = wqkv.reshape(2, 5, 128, 4096)
    sh["bqk"] = bqk
    sh["bv"] = np.ascontiguousarray(bq[:, 1152:1280].reshape(2, 1, 128))
    sh["snk"] = np.ascontiguousarray(np.repeat(inp["attn_sinks"].reshape(2, 16, 1), 64, axis=2).reshape(2, 1, 1024))
    sh["wo"] = np.ascontiguousarray(inp["attn_w_o"].reshape(2, 4, 4, 64, 1024).transpose(0, 1, 3, 2, 4).reshape(2, 4, 64, 4096))
    sh["bo"] = np.ascontiguousarray(inp["attn_b_o"].reshape(2, 1, 1024))
    sh["win"] = np.ascontiguousarray(inp["ssm_w_in"].reshape(2, 8, 128, 2, 512).transpose(0, 3, 2, 1, 4).reshape(2, 2, 128, 4096))
    sh["bin"] = np.ascontiguousarray(inp["ssm_b_in"].reshape(2, 8, 128).transpose(0, 2, 1))
    sh["dd"] = np.ascontiguousarray(inp["ssm_d"].reshape(2, 8, 128).transpose(0, 2, 1))
    sh["wglu"] = np.ascontiguousarray(inp["ssm_w_glu"].reshape(2, 8, 128, 2, 2, 512).transpose(0, 4, 3, 2, 1, 5).reshape(2, 4, 128, 4096))
    sh["bglu"] = np.ascontiguousarray(inp["ssm_b_glu"].reshape(2, 1, 2048))

    def smlay(a):
        return np.ascontiguousarray(a.reshape(2, 32, 2, 64).transpose(0, 2, 3, 1).reshape(2, 128, 32))
    sh["ldt"] = smlay(np.repeat(inp["ssm_log_dt"].reshape(2, 64, 1), 64, axis=2))
    sh["are"] = smlay(inp["ssm_a_re"])
    sh["aim"] = smlay(inp["ssm_a_im"])
    bm = np.zeros((2, 8, 128, 2, 4, 128), f)
    cm = np.zeros((2, 8, 128, 2, 4, 128), f)
    for part, (bsrc, csrc) in enumerate(((inp["ssm_b_re"], inp["ssm_c_re"]), (inp["ssm_b_im"], inp["ssm_c_im"]))):
        for q in range(32):
            kc, q4 = q // 4, q % 4
            for r in range(2):
                gi = 2 * q + r
                bm[:, kc, q4 * 32 + r * 16:q4 * 32 + r * 16 + 16, part, q4, r * 64:(r + 1) * 64] = bsrc[:, gi].transpose(0, 2, 1)
                cm[:, kc, r * 64:(r + 1) * 64, part, q4, q4 * 32 + r * 16:q4 * 32 + r * 16 + 16] = csrc[:, gi].transpose(0, 2, 1)
    sh["bm"] = bm.reshape(2, 8, 128, 1024)
    sh["cm"] = cm.reshape(2, 8, 128, 1024)
    wu = inp["ffn_w_up"]
    wup = np.zeros((4, 11, 128, 8, 512), f)
    for j in range(NJ):
        for which in range(2):
            c0_ = which * DFF + j * 128
            w = wu[:, :, c0_:c0_ + 128].reshape(4, 8, 128, 128).transpose(0, 2, 1, 3)
            mt = (j % 2) * 2 + which
            wup[:, j // 2, :, :, mt * 128:(mt + 1) * 128] = w
    sh["wup"] = wup.reshape(4, 11, 128, 4096)
    wd = np.zeros((4, 24, 128, 1024), f)
    wd[:, :NJ] = inp["ffn_w_down"].reshape(4, NJ, 128, 1024)
    sh["wdn"] = np.ascontiguousarray(wd.reshape(4, 6, 4, 128, 1024).transpose(0, 1, 3, 2, 4).reshape(4, 6, 128, 4096))
    sh["lng"] = np.ascontiguousarray(np.concatenate([inp["ln_gain"], inp["ln_bias"]], axis=-1).reshape(4, 2, 1, 2048))
    sh["ident"] = np.eye(128, dtype=f)
    sh["jgrid"] = np.ascontiguousarray(np.tile(np.arange(512, dtype=f), (128, 1)))
    invf = (10000.0 ** (-(np.arange(128) % 32) / 32.0)).astype(np.float64)
    sign = np.where((np.arange(128) % 64) < 32, -1.0, 1.0)
    sh["ropec"] = np.stack([invf / TWO_PI, TWO_PI * sign], axis=1).astype(f)
    return sh


_NC_CACHE = {}


def kernel(**inputs):
    inp = {k: np.asarray(v) for k, v in inputs.items()}
    n_ptiles = inp["x_prompt"].shape[1] // TT
    sh = _prep_shared(inp)
    sh["pos"] = np.arange(n_ptiles * TT + 64, dtype=np.float32).reshape(1, -1)
    sh["pos"][0, n_ptiles * TT:] = 4096.0 + np.arange(64)
    key = n_ptiles
    if key not in _NC_CACHE:
        _NC_CACHE[key] = build(n_ptiles=n_ptiles)
    nc = _NC_CACHE[key]

    def smst(a):
        return np.ascontiguousarray(a.reshape(32, 2, 64).transpose(1, 2, 0).reshape(128, 32))
    in_maps = []
    for b in range(8):
        m = dict(sh)
        m["xp"] = np.ascontiguousarray(inp["x_prompt"][b])
        m["xs"] = np.ascontiguousarray(inp["x_sample"][b])
        m["ck"] = np.ascontiguousarray(inp["cache_k"][:, b].reshape(2, 128, 128))
        m["cv"] = np.ascontiguousarray(inp["cache_v"][:, b].reshape(2, 128, 128))
        m["sre"] = np.stack([smst(inp["state_ssm_re"][s, b]) for s in range(2)])
        m["sim"] = np.stack([smst(inp["state_ssm_im"][s, b]) for s in range(2)])
        in_maps.append(m)
    ncores = int(os.environ.get("KCORES", "8"))
    res = run_bass_kernel_spmd(nc, in_maps[:ncores], core_ids=list(range(ncores)))
    R = list(res.results)
    while len(R) < 8:
        R.append(R[0])

    def gather(name, shape):
        return np.stack([np.asarray(R[b][name]).reshape(shape) for b in range(8)], axis=0)
    yp = gather("yp", (n_ptiles * TT, D))
    ys = gather("ys", (64, D))

    def kv(name):
        return np.ascontiguousarray(gather(name, (2, 128, 2, 64)).transpose(1, 0, 2, 3, 4))

    def stt(name):
        return np.ascontiguousarray(gather(name, (2, 64, 64)).transpose(1, 0, 2, 3))
    return (yp.astype(np.float32), ys.astype(np.float32), kv("nkp"), kv("nvp"), stt("nrp"), stt("nip"),
            kv("nks"), kv("nvs"), stt("nrs"), stt("nis"))
```

```python
import contextlib
import math
import os
import numpy as np
import concourse.bass as bass
import concourse.mybir as mybir
from concourse.bass_utils import run_bass_kernel_spmd

F32 = mybir.dt.float32
BF16 = mybir.dt.bfloat16
I32 = mybir.dt.int32
AF = mybir.ActivationFunctionType
ALU = mybir.AluOpType

ENGS = ("pe", "act", "dve", "pool", "sp")

D = 1024
KC = 8
TT = 512
DFF = 2816
NJ = 22
ALPHA = 8.0 ** 0.25
EPS = 1e-5
TWO_PI = 2.0 * math.pi


class Op:
    __slots__ = ("eng", "emit", "deps", "token", "signal", "is_dma", "ring_wait", "nop", "tag")

    def __init__(self, eng, emit):
        self.eng = eng
        self.emit = emit
        self.deps = []
        self.token = None
        self.signal = False
        self.is_dma = False
        self.ring_wait = None
        self.nop = False


class _Rec:
    def __init__(self):
        self.calls = []

    def __getattr__(self, name):
        def f(*a, **k):
            self.calls.append((name, a, k))
            return None
        return f


class Sched:
    NRING = 6

    def __init__(self, nc):
        self.nc = nc
        self.streams = {e: [] for e in ENGS}
        self.last_w = {}
        self.readers = {}
        self.dma_count = {e: 0 for e in ENGS}
        self.dma_ring_ops = {e: [] for e in ENGS}
        self.out_dmas = []
        self.bar_lasts = []
        self.cur_tag = "init"
        self.annotate = bool(os.environ.get("KANNOT"))

    def barrier(self):
        lasts = []
        for e in ENGS:
            for o in reversed(self.streams[e]):
                if not o.is_dma and not o.nop:
                    lasts.append(o)
                    break
            lasts += self.dma_ring_ops[e][-self.NRING:]
        self.bar_lasts = lasts
        for e in ("pe", "act", "dve"):
            b = self.op(e, lambda eng: None)
            b.nop = True
            b.deps = list(lasts)

    def op(self, eng, emit, reads=(), writes=(), dma=False, out=False, arena=False):
        rec = _Rec()
        emit(rec)
        o = Op(eng, rec.calls)
        o.tag = self.cur_tag
        o.is_dma = dma
        deps = set(self.bar_lasts) if arena else set()
        for k in reads:
            w = self.last_w.get(k)
            if w is not None:
                deps.add(w)
            if isinstance(k, tuple) and k[0] == "ps":
                for r in self.readers.get(k, ()):
                    if r.eng != eng:
                        deps.add(r)
        for k in writes:
            w = self.last_w.get(k)
            if w is not None:
                deps.add(w)
            for r in self.readers.get(k, ()):
                deps.add(r)
        o.deps = list(deps)
        for k in reads:
            self.readers.setdefault(k, []).append(o)
        for k in writes:
            self.last_w[k] = o
            self.readers[k] = []
        if dma:
            i = self.dma_count[eng]
            self.dma_count[eng] += 1
            ring = self.dma_ring_ops[eng]
            if i >= self.NRING:
                o.ring_wait = ring[i - self.NRING]
            ring.append(o)
            o.token = (("dma", eng, i % self.NRING), 16 * (i // self.NRING + 1))
            if out:
                self.out_dmas.append(o)
        self.streams[eng].append(o)
        return o

    def emit_all(self):
        nc = self.nc
        fin = self.op("sp", lambda e: None)
        fin.deps = list(self.out_dmas)
        for e in ENGS:
            for o in self.streams[e]:
                for d in o.deps:
                    d.signal = True
        for e in ENGS:
            c = 0
            for o in self.streams[e]:
                if o.is_dma:
                    continue
                if o.signal:
                    c += 1
                    o.token = (("eng", e), c)
        semkeys = set()
        for e in ENGS:
            for o in self.streams[e]:
                if o.token is not None:
                    semkeys.add(o.token[0])
        semkeys = sorted(semkeys, key=str)
        with contextlib.ExitStack() as st:
            sems = {}
            for i, k in enumerate(semkeys):
                sems[k] = st.enter_context(nc.semaphore("s%d" % i))
            block = st.enter_context(nc.Block())
            handles = {"pe": "tensor", "act": "scalar", "dve": "vector", "pool": "gpsimd", "sp": "sync"}

            def make(e):
                def body(eng):
                    waited = {}
                    for o in self.streams[e]:
                        need = {}
                        dl = list(o.deps)
                        if o.ring_wait is not None:
                            dl.append(o.ring_wait)
                        for d in dl:
                            k, v = d.token
                            if need.get(k, 0) < v:
                                need[k] = v
                        for k, v in need.items():
                            if waited.get(k, 0) < v:
                                eng.wait_ge(sems[k], v)
                                waited[k] = v
                        ins = None
                        for (nm_, a_, k_) in o.emit:
                            ins = getattr(eng, nm_)(*a_, **k_)
                            if self.annotate:
                                ins.annotate(o.tag)
                        if o.is_dma:
                            ins.then_inc(sems[o.token[0]], 16)
                        elif o.signal:
                            ins.then_inc(sems[o.token[0]], 1)
                return body

            for e in ENGS:
                if self.streams[e]:
                    getattr(block, handles[e])(make(e))


import os
_SKIP = set(os.environ.get("KSKIP", "").split(","))


def build(n_ptiles=8, with_sample=True, depth=4):
    depth = int(os.environ.get("KDEPTH", depth))
    nc = bass.Bass("TRN2", target_bir_lowering=False)
    S = Sched(nc)
    NTOKP = n_ptiles * TT

    def din(name, shape, dt=F32):
        return nc.dram_tensor(name, list(shape), dt, kind="ExternalInput").ap()

    def dout(name, shape, dt=F32):
        return nc.dram_tensor(name, list(shape), dt, kind="ExternalOutput").ap()

    xp_d = din("xp", [NTOKP, D])
    xs_d = din("xs", [64, D])
    ck_d = din("ck", [2, 128, 128])
    cv_d = din("cv", [2, 128, 128])
    sre_d = din("sre", [2, 128, 32])
    sim_d = din("sim", [2, 128, 32])
    wqkv_d = din("wqkv", [2, 5, 128, 4096])
    bqk_d = din("bqk", [2, 128, 18])
    bv_d = din("bv", [2, 1, 128])
    snk_d = din("snk", [2, 1, 1024])
    wo_d = din("wo", [2, 4, 64, 4096])
    bo_d = din("bo", [2, 1, 1024])
    win_d = din("win", [2, 2, 128, 4096])
    bin_d = din("bin", [2, 128, 8])
    dd_d = din("dd", [2, 128, 8])
    wglu_d = din("wglu", [2, 4, 128, 4096])
    bglu_d = din("bglu", [2, 1, 2048])
    ldt_d = din("ldt", [2, 128, 32])
    are_d = din("are", [2, 128, 32])
    aim_d = din("aim", [2, 128, 32])
    bm_d = din("bm", [2, 8, 128, 1024])
    cm_d = din("cm", [2, 8, 128, 1024])
    wup_d = din("wup", [4, 11, 128, 4096])
    wdn_d = din("wdn", [4, 6, 128, 4096])
    lng_d = din("lng", [4, 2, 1, 2048])
    ident_d = din("ident", [128, 128])
    jgrid_d = din("jgrid", [128, 512])
    pos_d = din("pos", [1, NTOKP + 64])
    rc_d = din("ropec", [128, 2])

    yp_d = dout("yp", [NTOKP, D])
    ys_d = dout("ys", [64, D])
    nkp_d = dout("nkp", [2, 128, 128])
    nvp_d = dout("nvp", [2, 128, 128])
    nrp_d = dout("nrp", [2, 32, 128])
    nip_d = dout("nip", [2, 32, 128])
    nks_d = dout("nks", [2, 128, 128])
    nvs_d = dout("nvs", [2, 128, 128])
    nrs_d = dout("nrs", [2, 32, 128])
    nis_d = dout("nis", [2, 32, 128])
    tab_d = nc.dram_tensor("tabs", [2, 32, 128, 2560], BF16, kind="Internal").ap()

    st = contextlib.ExitStack()
    with st:
        def sb(name, shape, dt):
            return st.enter_context(nc.sbuf_tensor(name, list(shape), dt))

        xres = sb("xres", [128, 4, D], F32)
        xT = sb("xT", [128, KC, TT], BF16)
        lng = [sb("lng0", [128, 2048], F32)]
        stats = sb("stats", [128, 2, 6], F32)
        mv = sb("mv", [128, 2], F32)
        rstd = sb("rstd", [128, 1], F32)
        ident = sb("ident_sb", [128, 128], F32)
        ones_bf = sb("ones_bf", [128, 128], BF16)
        e0 = sb("e0", [128, 128], BF16)
        NSLOT = 8
        ring = [sb("ring%d" % i, [128, 4096], BF16) for i in range(NSLOT)]
        cosT = sb("cosT", [128, TT], F32)
        sinT = sb("sinT", [128, TT], F32)
        ropec = sb("ropec_sb", [128, 2], F32)
        KTb = [sb("KT%d" % i, [128, 128 + TT], BF16) for i in range(2)]
        Vc = [sb("Vc%d" % i, [64, 10, 128], BF16) for i in range(2)]
        kf32 = sb("kf32", [128, 128], F32)
        ktr = sb("ktr", [128, 128], F32)
        kfull = sb("kfull", [128, 128], F32)
        vfull = sb("vfull", [128, 128], F32)
        vf32 = vfull
        rowf = sb("rowf", [1, 1024], F32)
        c0 = [sb("c0_%d" % i, [128, 32, 2], F32) for i in range(2)]
        abl = [sb("abl%d" % i, [128, 32, 4], F32) for i in range(2)]
        abar = [sb("abar%d" % i, [128, 32, 2], F32) for i in range(2)]
        tiny = sb("tiny", [128, 8], F32)
        tinyp = [sb("tinyp%d" % i, [128, 2], F32) for i in range(2)]
        send = sb("send", [128, 128], F32)
        sendT = ktr[0:64, :]
        binc = sb("binc", [128, 8], F32)
        ddc = sb("ddc", [128, 8], F32)
        bqk = sb("bqk_sb", [128, 18], F32)
        sm = [sb("sm%d" % i, [128, 32], F32) for i in range(16)]
        smi = sb("smi", [128, 32], I32)
        NBF, NF32 = 31744, 5632
        arena_bf = sb("arena_bf", [128, NBF], BF16)
        arena_f = sb("arena_f", [128, NF32], F32)
        arena_i = sb("arena_i", [128, 512], I32)
        aoff = {"bf": 0, "f": 0}

        def areset():
            aoff["bf"] = 0
            aoff["f"] = 0

        def abf(n, parts=128, shape=None):
            o = aoff["bf"]; aoff["bf"] += n
            assert aoff["bf"] <= NBF, aoff
            v = arena_bf[0:parts, o:o + n]
            if shape:
                v = v.rearrange("p (a b) -> p a b", a=shape[0])
            return v

        def af(n, parts=128, shape=None):
            o = aoff["f"]; aoff["f"] += n
            assert aoff["f"] <= NF32, aoff
            v = arena_f[0:parts, o:o + n]
            if shape:
                v = v.rearrange("p (a b) -> p a b", a=shape[0])
            return v
        areset()
        jg = af(512)
        cw = [af(512) for i in range(10)]
        cwi = arena_i[:, 0:512]
        ctw = abf(2560)

        psb = [st.enter_context(nc.psum_tensor("ps%d" % i, [128, 512], F32)) for i in range(8)]
        pctr = [0]

        def ps_next():
            i = pctr[0] % 6
            pctr[0] += 1
            return psb[i], ("ps", i)

        rctr = [0]

        def wload(dram_ap, nparts=128, ncols=4096):
            i = rctr[0] % NSLOT
            rctr[0] += 1
            slot = ring[i]
            key = ("ring", i)
            S.op("pool", lambda e: e.dma_start(out=slot[0:nparts, 0:ncols].rearrange("p (a b) -> p a b", b=1024), in_=dram_ap.rearrange("p (a b) -> p a b", b=1024)), writes=[key], dma=True)
            return slot, key

        def dma_in(dst, src, wkeys, arena=False):
            S.op("sp", lambda e: e.dma_start(out=dst, in_=src), writes=wkeys, dma=True, arena=arena)

        def dma_out(dst, src, rkeys):
            S.op("sp", lambda e: e.dma_start(out=dst, in_=src), reads=rkeys, dma=True, out=True)

        def dve(fn, reads, writes):
            return S.op("dve", fn, reads=reads, writes=writes)

        def act(fn, reads, writes):
            return S.op("act", fn, reads=reads, writes=writes)

        def pe(fn, reads, writes):
            return S.op("pe", fn, reads=reads, writes=writes)

        dma_in(ident[:], ident_d, ["ident"])
        dma_in(jg[:], jgrid_d, ["jg"], arena=True)
        dma_in(ropec[:], rc_d, ["ropec"])
        S.op("pool", lambda e: e.memset(ones_bf[:], 1.0), writes=["ones"])
        S.op("pool", lambda e: e.memset(e0[:], 0.0), writes=["e0"])
        S.op("pool", lambda e: e.memset(e0[0:1, :], 1.0), writes=["e0"])
        S.op("pool", lambda e: e.memset(send[:], 0.0), writes=["send"])
        for l in range(2):
            S.op("pool", (lambda l: lambda e: e.memset(c0[l][:], 0.0))(l), writes=[("c0", l)])

        def range_sin(out_ap, y_ap, tmp_i, tmp_f, keys_r, key_w, ki, kf, scale=TWO_PI, shape_fix=None):
            dve(lambda e: e.tensor_copy(out=tmp_i, in_=y_ap), keys_r, [ki])
            dve(lambda e: e.tensor_copy(out=tmp_f, in_=tmp_i), [ki], [kf])
            dve(lambda e: e.tensor_tensor(out=tmp_f, in0=y_ap, in1=tmp_f, op=ALU.subtract), keys_r + [kf], [kf])
            act(lambda e: e.activation(out=out_ap, in_=tmp_f, func=AF.Sin, scale=scale), [kf], [key_w])

        n_ssm = 0 if "setup" in _SKIP else 2
        for s in range(n_ssm):
            ldt, are, aim, dt, lr, li = sm[0], sm[1], sm[2], sm[3], sm[4], sm[5]
            dma_in(ldt[:], ldt_d[s], ["sm0"])
            dma_in(are[:], are_d[s], ["sm1"])
            dma_in(aim[:], aim_d[s], ["sm2"])
            act(lambda e: e.activation(out=dt[:], in_=ldt[:], func=AF.Exp), ["sm0"], ["sm3"])
            dve(lambda e: e.tensor_tensor(out=lr[:], in0=dt[:], in1=are[:], op=ALU.mult), ["sm3", "sm1"], ["sm4"])
            dve(lambda e: e.tensor_tensor(out=li[:], in0=dt[:], in1=aim[:], op=ALU.mult), ["sm3", "sm2"], ["sm5"])
            ys, yc, sn, cs, mg = sm[6], sm[7], sm[8], sm[9], sm[10]
            dve(lambda e: e.tensor_scalar(out=ys[:], in0=li[:], scalar1=1.0 / TWO_PI, scalar2=None, op0=ALU.mult), ["sm5"], ["sm6"])
            dve(lambda e: e.tensor_scalar(out=yc[:], in0=ys[:], scalar1=0.25, scalar2=None, op0=ALU.add), ["sm6"], ["sm7"])
            range_sin(sn[:], ys[:], smi[:], sm[11][:], ["sm6"], "sm8", "smi", "sm11")
            range_sin(cs[:], yc[:], smi[:], sm[11][:], ["sm7"], "sm9", "smi", "sm11")
            act(lambda e: e.activation(out=mg[:], in_=lr[:], func=AF.Exp), ["sm4"], ["sm10"])
            abr, abi = sm[12], sm[13]
            dve(lambda e: e.tensor_tensor(out=abr[:], in0=mg[:], in1=cs[:], op=ALU.mult), ["sm10", "sm9"], ["sm12"])
            dve(lambda e: e.tensor_tensor(out=abi[:], in0=mg[:], in1=sn[:], op=ALU.mult), ["sm10", "sm8"], ["sm13"])
            nr, den, fr, fi = sm[14], sm[15], sm[6], sm[7]
            dve(lambda e: e.tensor_scalar(out=nr[:], in0=abr[:], scalar1=-1.0, scalar2=None, op0=ALU.add), ["sm12"], ["sm14"])
            dve(lambda e: e.tensor_tensor(out=den[:], in0=are[:], in1=are[:], op=ALU.mult), ["sm1"], ["sm15"])
            dve(lambda e: e.tensor_tensor(out=sm[11][:], in0=aim[:], in1=aim[:], op=ALU.mult), ["sm2"], ["sm11"])
            dve(lambda e: e.tensor_tensor(out=den[:], in0=den[:], in1=sm[11][:], op=ALU.add), ["sm15", "sm11"], ["sm15"])
            dve(lambda e: e.reciprocal(out=den[:], in_=den[:]), ["sm15"], ["sm15"])
            dve(lambda e: e.tensor_tensor(out=fr[:], in0=nr[:], in1=are[:], op=ALU.mult), ["sm14", "sm1"], ["sm6"])
            dve(lambda e: e.tensor_tensor(out=sm[11][:], in0=abi[:], in1=aim[:], op=ALU.mult), ["sm13", "sm2"], ["sm11"])
            dve(lambda e: e.tensor_tensor(out=fr[:], in0=fr[:], in1=sm[11][:], op=ALU.add), ["sm6", "sm11"], ["sm6"])
            dve(lambda e: e.tensor_tensor(out=fr[:], in0=fr[:], in1=den[:], op=ALU.mult), ["sm6", "sm15"], ["sm6"])
            dve(lambda e: e.tensor_tensor(out=fi[:], in0=abi[:], in1=are[:], op=ALU.mult), ["sm13", "sm1"], ["sm7"])
            dve(lambda e: e.tensor_tensor(out=sm[11][:], in0=nr[:], in1=aim[:], op=ALU.mult), ["sm14", "sm2"], ["sm11"])
            dve(lambda e: e.tensor_tensor(out=fi[:], in0=fi[:], in1=sm[11][:], op=ALU.subtract), ["sm7", "sm11"], ["sm7"])
            dve(lambda e: e.tensor_tensor(out=fi[:], in0=fi[:], in1=den[:], op=ALU.mult), ["sm7", "sm15"], ["sm7"])
            for bi, blv in enumerate((512.0, 64.0)):
                yb, ybc, sb_, cb_, mb_ = sm[8], sm[9], sm[10], sm[12], sm[13]
                dve((lambda blv: lambda e: e.tensor_scalar(out=yb[:], in0=li[:], scalar1=blv / TWO_PI, scalar2=None, op0=ALU.mult))(blv), ["sm5"], ["sm8"])
                dve(lambda e: e.tensor_scalar(out=ybc[:], in0=yb[:], scalar1=0.25, scalar2=None, op0=ALU.add), ["sm8"], ["sm9"])
                range_sin(sb_[:], yb[:], smi[:], sm[11][:], ["sm8"], "sm10", "smi", "sm11")
                range_sin(cb_[:], ybc[:], smi[:], sm[11][:], ["sm9"], "sm12", "smi", "sm11")
                act((lambda blv: lambda e: e.activation(out=mb_[:], in_=lr[:], func=AF.Exp, scale=blv))(blv), ["sm4"], ["sm13"])
                dve((lambda bi: lambda e: e.tensor_tensor(out=abl[s][:, :, 2 * bi], in0=mb_[:], in1=cb_[:], op=ALU.mult))(bi), ["sm13", "sm12"], [("abl", s)])
                dve((lambda bi: lambda e: e.tensor_tensor(out=abl[s][:, :, 2 * bi + 1], in0=mb_[:], in1=sb_[:], op=ALU.mult))(bi), ["sm13", "sm10"], [("abl", s)])
            nlr = sm[14]
            dve(lambda e: e.tensor_scalar(out=nlr[:], in0=lr[:], scalar1=-1.0, scalar2=None, op0=ALU.mult), ["sm4"], ["nlr", "sm14"])
            for q in range(32):
                kc, q4 = q // 4, q % 4
                yj, yjc, sj, cj, mn, mp = cw[0], cw[1], cw[2], cw[3], cw[4], cw[5]
                lic, lrc, frc, fic = li[:, q:q + 1], lr[:, q:q + 1], fr[:, q:q + 1], fi[:, q:q + 1]
                dve(lambda e: e.tensor_scalar(out=yj[:], in0=jg[:], scalar1=lic, scalar2=1.0 / TWO_PI, op0=ALU.mult, op1=ALU.mult), ["jg", "sm5"], ["cw0"])
                dve(lambda e: e.tensor_scalar(out=yjc[:], in0=yj[:], scalar1=0.25, scalar2=None, op0=ALU.add), ["cw0"], ["cw1"])
                range_sin(sj[:], yj[:], cwi[:], cw[6][:], ["cw0"], "cw2", "cwi", "cw6")
                range_sin(cj[:], yjc[:], cwi[:], cw[6][:], ["cw1"], "cw3", "cwi", "cw6")
                act(lambda e: e.activation(out=mp[:], in_=jg[:], func=AF.Exp, scale=lrc), ["jg", "sm4"], ["cw5"])
                act(lambda e: e.activation(out=mn[:], in_=jg[:], func=AF.Exp, scale=nlr[:, q:q + 1]), ["jg", "nlr"], ["cw4"])
                er, ei = cw[7], cw[8]
                dve(lambda e: e.tensor_tensor(out=er[:], in0=mn[:], in1=cj[:], op=ALU.mult), ["cw4", "cw3"], ["cw7"])
                dve(lambda e: e.scalar_tensor_tensor(out=ei[:], in0=mn[:], scalar=-1.0, in1=sj[:], op0=ALU.mult, op1=ALU.mult), ["cw4", "cw2"], ["cw8"])
                t9 = cw[9]
                dve(lambda e: e.tensor_scalar(out=t9[:], in0=ei[:], scalar1=fic, scalar2=None, op0=ALU.mult), ["cw8", "sm7"], ["cw9"])
                dve(lambda e: e.scalar_tensor_tensor(out=ctw[:, 0:512], in0=er[:], scalar=frc, in1=t9[:], op0=ALU.mult, op1=ALU.subtract), ["cw7", "cw9", "sm6"], ["ctw"])
                dve(lambda e: e.tensor_scalar(out=t9[:], in0=er[:], scalar1=fic, scalar2=None, op0=ALU.mult), ["cw7", "sm7"], ["cw9"])
                dve(lambda e: e.scalar_tensor_tensor(out=ctw[:, 512:1024], in0=ei[:], scalar=frc, in1=t9[:], op0=ALU.mult, op1=ALU.add), ["cw8", "cw9", "sm6"], ["ctw"])
                dve(lambda e: e.tensor_tensor(out=ctw[:, 1024:1536], in0=mp[:], in1=cj[:], op=ALU.mult), ["cw5", "cw3"], ["ctw"])
                dve(lambda e: e.scalar_tensor_tensor(out=ctw[:, 1536:2048], in0=mp[:], scalar=-1.0, in1=sj[:], op0=ALU.mult, op1=ALU.mult), ["cw5", "cw2"], ["ctw"])
                dve(lambda e: e.scalar_tensor_tensor(out=ctw[:, 2048:2560], in0=mp[:], scalar=-1.0, in1=cj[:], op0=ALU.mult, op1=ALU.mult), ["cw5", "cw3"], ["ctw"])
                S.op("sp", lambda e: e.dma_start(out=tab_d[s, q], in_=ctw[:]), reads=["ctw"], writes=[("tabd", s, q)], dma=True)

        def transpose_to_xT(nb, bl):
            S.cur_tag = "xT"
            ntok = nb * bl
            for kc in range(KC):
                p, pk = ps_next()

                def f(e, kc=kc, p=p):
                    r = None
                    for blk in range(nb):
                        r = e.transpose(p[:, blk * bl:(blk + 1) * bl], xres[0:bl, blk, kc * 128:(kc + 1) * 128], ident[0:bl, 0:bl])
                    return r
                pe(f, [("xr", b_) for b_ in range(4)] + ["ident"], [pk])
                act((lambda kc, p: lambda e: e.activation(out=xT[:, kc, 0:ntok], in_=p[:, 0:ntok], func=AF.Copy))(kc, p), [pk], [("xT", kc)])

        lnctr = [0]

        def layer_norm(i, sub, nb, bl):
            S.cur_tag = "ln"
            slot = lng[0]
            lk = ("lng", 0)
            dma_in(slot[:], lng_d[i, sub].partition_broadcast(128), [lk])
            for blk in range(nb):
                xk = ("xr", blk)
                for h in range(2):
                    dve((lambda blk, h: lambda e: e.bn_stats(out=stats[0:bl, h, :], in_=xres[0:bl, blk, h * 512:(h + 1) * 512]))(blk, h), [xk], ["stats"])
                dve(lambda e: e.bn_aggr(out=mv[0:bl, :], in_=stats[0:bl, :, :].rearrange("p a b -> p (a b)")), ["stats"], ["mv"])
                act(lambda e: e.activation(out=rstd[0:bl, :], in_=mv[0:bl, 1:2], func=AF.Sqrt, bias=EPS, scale=1.0), ["mv"], ["rstd"])
                dve(lambda e: e.reciprocal(out=rstd[0:bl, :], in_=rstd[0:bl, :]), ["rstd"], ["rstd"])
                dve((lambda blk: lambda e: e.tensor_scalar(out=xres[0:bl, blk, :], in0=xres[0:bl, blk, :], scalar1=mv[0:bl, 0:1], scalar2=rstd[0:bl, 0:1], op0=ALU.subtract, op1=ALU.mult))(blk), [xk, "mv", "rstd"], [xk])
                dve((lambda blk: lambda e: e.tensor_tensor(out=xres[0:bl, blk, :], in0=xres[0:bl, blk, :], in1=slot[0:bl, 0:1024], op=ALU.mult))(blk), [xk, lk], [xk])
                dve((lambda blk: lambda e: e.tensor_tensor(out=xres[0:bl, blk, :], in0=xres[0:bl, blk, :], in1=slot[0:bl, 1024:2048], op=ALU.add))(blk), [xk, lk], [xk])

        def residual_evac(p, pk, blk, bl, c0_, n):
            dve(lambda e: e.scalar_tensor_tensor(out=xres[0:bl, blk, c0_:c0_ + n], in0=xres[0:bl, blk, c0_:c0_ + n], scalar=ALPHA, in1=p[0:bl, 0:n], op0=ALU.mult, op1=ALU.add), [("xr", blk), pk], [("xr", blk)])

        def ffn(i, nb, bl):
            ntok = nb * bl
            S.cur_tag = "ffn"
            S.barrier()
            areset()
            g = abf(NJ * TT, shape=(NJ, TT))
            sil = [af(TT) for _ in range(2)]
            for pc in range(11):
                slot, sk = wload(wup_d[i, pc])
                sv = slot[:].rearrange("p (k c) -> p k c", k=KC)
                for jj in range(2):
                    j = 2 * pc + jj
                    pa, pak = ps_next()
                    pb, pbk = ps_next()

                    def f(e, sv=sv, jj=jj, pa=pa, pb=pb):
                        r = None
                        for which, p in ((0, pa), (1, pb)):
                            c = (2 * jj + which) * 128
                            for kc in range(KC):
                                r = e.matmul(p[:, 0:ntok], lhsT=sv[:, kc, c:c + 128], rhs=xT[:, kc, 0:ntok], start=(kc == 0), stop=(kc == KC - 1))
                        return r
                    pe(f, [sk] + [("xT", k) for k in range(KC)], [pak, pbk])
                    sl = sil[j % 2]
                    slk = ("sil", j % 2)
                    act((lambda pa, sl: lambda e: e.activation(out=sl[:, 0:ntok], in_=pa[:, 0:ntok], func=AF.Silu))(pa, sl), [pak], [slk])
                    dve((lambda pb, sl, j: lambda e: e.tensor_tensor(out=g[:, j, 0:ntok], in0=pb[:, 0:ntok], in1=sl[:, 0:ntok], op=ALU.mult))(pb, sl, j), [pbk, slk], [("g", j)])
            dsl = []
            for pc in range(6):
                slot, sk = wload(wdn_d[i, pc])
                dsl.append((slot[:].rearrange("p (k c) -> p k c", k=4), sk))
            for blk in range(nb):
                for half in range(2):
                    p, pk = ps_next()

                    def f(e, p=p, blk=blk, half=half):
                        r = None
                        for j in range(NJ):
                            sv, _ = dsl[j // 4]
                            r = e.matmul(p[0:bl, :], lhsT=g[:, j, blk * bl:(blk + 1) * bl], rhs=sv[:, j % 4, half * 512:(half + 1) * 512], start=(j == 0), stop=(j == NJ - 1))
                        return r
                    pe(f, [k for _, k in dsl] + [("g", j) for j in range(NJ)], [pk])
                    residual_evac(p, pk, blk, bl, half * 512, 512)

        def rope_tables(pos0, ntok):
            S.cur_tag = "rope"
            S.barrier()
            areset()
            rt = [af(TT) for _ in range(3)]
            rti = arena_i[:, 0:TT]
            dma_in(rt[0][:, 0:ntok], pos_d[:, pos0:pos0 + ntok].partition_broadcast(128), ["rt0"], arena=True)
            dve(lambda e: e.tensor_scalar(out=rt[1][:, 0:ntok], in0=rt[0][:, 0:ntok], scalar1=ropec[:, 0:1], scalar2=None, op0=ALU.mult), ["rt0", "ropec"], ["rt1"])
            dve(lambda e: e.tensor_scalar(out=rt[2][:, 0:ntok], in0=rt[1][:, 0:ntok], scalar1=0.25, scalar2=None, op0=ALU.add), ["rt1"], ["rt2"])
            dve(lambda e: e.tensor_copy(out=rti[:, 0:ntok], in_=rt[1][:, 0:ntok]), ["rt1"], ["rti"])
            dve(lambda e: e.tensor_copy(out=rt[0][:, 0:ntok], in_=rti[:, 0:ntok]), ["rti"], ["rt0"])
            dve(lambda e: e.tensor_tensor(out=rt[0][:, 0:ntok], in0=rt[1][:, 0:ntok], in1=rt[0][:, 0:ntok], op=ALU.subtract), ["rt1", "rt0"], ["rt0"])
            act(lambda e: e.activation(out=sinT[:, 0:ntok], in_=rt[0][:, 0:ntok], func=AF.Sin, scale=ropec[:, 1:2]), ["rt0", "ropec"], ["sinT"])
            dve(lambda e: e.tensor_copy(out=rti[:, 0:ntok], in_=rt[2][:, 0:ntok]), ["rt2"], ["rti"])
            dve(lambda e: e.tensor_copy(out=rt[0][:, 0:ntok], in_=rti[:, 0:ntok]), ["rti"], ["rt0"])
            dve(lambda e: e.tensor_tensor(out=rt[0][:, 0:ntok], in0=rt[2][:, 0:ntok], in1=rt[0][:, 0:ntok], op=ALU.subtract), ["rt2", "rt0"], ["rt0"])
            act(lambda e: e.activation(out=cosT[:, 0:ntok], in_=rt[0][:, 0:ntok], func=AF.Sin, scale=TWO_PI), ["rt0"], ["cosT"])

        def attention(l, nb, bl, first_prompt, is_sample, is_last_prompt):
            ntok = nb * bl
            nch = ntok // 64
            KT = KTb[l]
            V = Vc[l]
            kk, vk = ("KT", l), ("Vc", l)
            S.cur_tag = "attn"
            S.barrier()
            areset()
            rt = [af(TT) for _ in range(3)]
            rden = [af(TT, parts=64) for _ in range(2)]
            QT = abf(8 * TT, shape=(8, TT))
            AO = abf(16 * TT, parts=64, shape=(16, TT))
            PT = [abf(TT, parts=64) for _ in range(6)]
            bvrow = abf(128)
            borow = abf(1024)
            esink = abf(1024)
            dve(lambda e: e.memset(bvrow[:, :], 0.0), [], ["bvrow"])
            dve(lambda e: e.memset(borow[:, :], 0.0), [], ["borow"])
            dve(lambda e: e.memset(esink[:, :], 0.0), [], ["esink"])
            dma_in(bqk[:], bqk_d[l], ["bqk"])
            dma_in(rowf[:, 0:128], bv_d[l], ["rowf"])
            dve(lambda e: e.tensor_copy(out=bvrow[0:1, :], in_=rowf[:, 0:128]), ["rowf"], ["bvrow"])
            dma_in(rowf[:, 0:1024], bo_d[l], ["rowf"])
            dve(lambda e: e.tensor_copy(out=borow[0:1, :], in_=rowf[:, 0:1024]), ["rowf"], ["borow"])
            dma_in(rowf[:, 0:1024], snk_d[l], ["rowf"])
            act(lambda e: e.activation(out=esink[0:1, :], in_=rowf[:, 0:1024], func=AF.Exp), ["rowf"], ["esink"])
            if is_sample and "a_cache" not in _SKIP:
                dma_in(kf32[:], ck_d[l], ["kf32"])
                p, pk = ps_next()
                pe(lambda e: e.transpose(p[:, 0:128], kf32[:], ident[:]), ["kf32", "ident"], [pk])
                act(lambda e: e.activation(out=KT[:, 0:128], in_=p[:, 0:128], func=AF.Copy), [pk], [kk])
                dve(lambda e: e.tensor_copy(out=kfull[:, 0:64], in_=p[:, 64:128]), [pk], ["kfull"])
                dma_in(vfull[0:64, :], cv_d[l, 64:128, :], ["vfull"])
                dma_in(rt[2][0:64, 0:256].rearrange("p (c f) -> p c f", c=2), cv_d[l].rearrange("(c p) f -> p c f", c=2), ["rt2"], arena=True)
                dve(lambda e: e.tensor_copy(out=V[:, 0:2, :], in_=rt[2][0:64, 0:256].rearrange("p (c f) -> p c f", c=2)), ["rt2"], [vk])

            slots = [wload(wqkv_d[l, pc]) for pc in range(5)]

            def wcols(mt):
                slot, sk = slots[mt // 4]
                return slot[:].rearrange("p (k c) -> p k c", k=KC), (mt % 4) * 128, sk
            xkeys = [("xT", k) for k in range(KC)]
            want_kf = (is_sample or is_last_prompt) and "a_kf" not in _SKIP
            for i in range(9 if "a_qk" not in _SKIP else 0):
                pa, pak = ps_next()
                pb, pbk = ps_next()

                def f(e, i=i, pa=pa, pb=pb):
                    r = None
                    for mt, p in ((2 * i, pa), (2 * i + 1, pb)):
                        sv, c, _ = wcols(mt)
                        for kc in range(KC):
                            r = e.matmul(p[:, 0:ntok], lhsT=sv[:, kc, c:c + 128], rhs=xT[:, kc, 0:ntok], start=(kc == 0), stop=(kc == KC - 1))
                    return r
                pe(f, xkeys + [wcols(2 * i)[2], wcols(2 * i + 1)[2]], [pak, pbk])
                dve((lambda i, pa: lambda e: e.scalar_tensor_tensor(out=rt[1][:, 0:ntok], in0=pa[:, 0:ntok], scalar=bqk[:, 2 * i:2 * i + 1], in1=cosT[:, 0:ntok], op0=ALU.add, op1=ALU.mult))(i, pa), [pak, "bqk", "cosT"], ["rt1"])
                dve((lambda i, pb: lambda e: e.scalar_tensor_tensor(out=rt[2][:, 0:ntok], in0=pb[:, 0:ntok], scalar=bqk[:, 2 * i + 1:2 * i + 2], in1=sinT[:, 0:ntok], op0=ALU.add, op1=ALU.mult))(i, pb), [pbk, "bqk", "sinT"], ["rt2"])
                if i < 8:
                    dve((lambda i: lambda e: e.tensor_tensor(out=QT[:, i, 0:ntok], in0=rt[1][:, 0:ntok], in1=rt[2][:, 0:ntok], op=ALU.add))(i), ["rt1", "rt2"], [("QT", i)])
                else:
                    dve(lambda e: e.tensor_tensor(out=KT[:, 128:128 + ntok], in0=rt[1][:, 0:ntok], in1=rt[2][:, 0:ntok], op=ALU.add), ["rt1", "rt2"], [kk])
                    if want_kf and is_sample:
                        dve(lambda e: e.tensor_tensor(out=kfull[:, 64:128], in0=rt[1][:, 0:64], in1=rt[2][:, 0:64], op=ALU.add), ["rt1", "rt2"], ["kfull"])
                        p, pk = ps_next()
                        pe(lambda e: e.transpose(p[:, 0:128], kfull[:, :], ident[:]), ["kfull", "ident"], [pk])
                        dve(lambda e: e.tensor_copy(out=ktr[:, :], in_=p[:, 0:128]), [pk], ["ktr"])
                        dma_out(nks_d[l], ktr[:, :], ["ktr"])
                    elif want_kf:
                        dve(lambda e: e.tensor_tensor(out=kf32[:, :], in0=rt[1][:, ntok - 128:ntok], in1=rt[2][:, ntok - 128:ntok], op=ALU.add), ["rt1", "rt2"], ["kf32"])
                        p, pk = ps_next()
                        pe(lambda e: e.transpose(p[:, 0:128], kf32[:, :], ident[:]), ["kf32", "ident"], [pk])
                        dve(lambda e: e.tensor_copy(out=ktr[:, :], in_=p[:, 0:128]), [pk], ["ktr"])
                        dma_out(nkp_d[l], ktr[:, :], ["ktr"])
            svv, cv_, vsk = wcols(18)
            for blk in range(nb if "a_v" not in _SKIP else 0):
                p, pk = ps_next()

                def f(e, p=p, blk=blk):
                    for kc in range(KC):
                        e.matmul(p[0:bl, 0:128], lhsT=xT[:, kc, blk * bl:(blk + 1) * bl], rhs=svv[:, kc, cv_:cv_ + 128], start=(kc == 0), stop=False)
                    return e.matmul(p[0:bl, 0:128], lhsT=e0[:, 0:bl], rhs=bvrow[:, :], start=False, stop=True)
                pe(f, xkeys + [vsk, "bvrow", "e0"], [pk])
                for hh in range(bl // 64):
                    act((lambda p, blk, hh: lambda e: e.activation(out=V[:, 2 + (blk * bl) // 64 + hh, :], in_=p[64 * hh:64 * hh + 64, 0:128], func=AF.Copy))(p, blk, hh), [pk], [vk])
                if want_kf and blk == nb - 1:
                    if is_sample:
                        act((lambda p: lambda e: e.activation(out=vfull[64:128, :], in_=p[0:64, 0:128], func=AF.Copy))(p), [pk], ["vfull"])
                        dma_out(nvs_d[l], vfull[:, :], ["vfull"])
                    else:
                        dve((lambda p: lambda e: e.tensor_copy(out=vf32[:, :], in_=p[:, 0:128]))(p), [pk], ["vfull"])
                        dma_out(nvp_d[l], vf32[:, :], ["vfull"])
            ptc = [0]
            for c in range(nch if "a_sc" not in _SKIP else 0):
                for kvh in range(2):
                    r0 = kvh * 64
                    kbs = [kb for kb in range(3) if not (first_prompt and c + kb < 2)]
                    pts = []
                    for kb in kbs:
                        kbuf = c + kb
                        p, pk = ps_next()
                        pe((lambda p, kbuf, c, r0: lambda e: e.matmul(p[0:64, :].rearrange("p (h q) -> p h q", h=8), lhsT=KT[r0:r0 + 64, kbuf * 64:(kbuf + 1) * 64], rhs=QT[r0:r0 + 64, :, c * 64:(c + 1) * 64], start=True, stop=True))(p, kbuf, c, r0), [kk] + [("QT", i) for i in range(8)], [pk])
                        pt = PT[ptc[0] % 6]
                        ptk = ("PT", ptc[0] % 6)
                        ptc[0] += 1
                        act((lambda p, pt: lambda e: e.activation(out=pt[:, :], in_=p[0:64, :], func=AF.Exp, scale=0.125))(p, pt), [pk], [ptk])
                        pts.append((pt, ptk, kbuf))
                    pd, pdk = ps_next()

                    def fd(e, pd=pd, pts=pts, kvh=kvh):
                        for n, (pt, _, _) in enumerate(pts):
                            e.matmul(pd[0:64, :], lhsT=ones_bf[0:64, 0:64], rhs=pt[:, :], start=(n == 0), stop=False)
                        return e.matmul(pd[0:64, :], lhsT=e0[0:64, 0:64], rhs=esink[0:64, kvh * 512:(kvh + 1) * 512], start=False, stop=True)
                    pe(fd, [k for _, k, _ in pts] + ["ones", "e0", "esink"], [pdk])
                    rd = rden[(c * 2 + kvh) % 2]
                    rdk = ("rden", (c * 2 + kvh) % 2)
                    act((lambda pd, rd: lambda e: e.activation(out=rd[:, :], in_=pd[0:64, :], func=AF.Ln))(pd, rd), [pdk], [rdk])
                    act((lambda rd: lambda e: e.activation(out=rd[:, :], in_=rd[:, :], func=AF.Exp, scale=-1.0))(rd), [rdk], [rdk])
                    po, pok = ps_next()

                    def fo(e, po=po, pts=pts, r0=r0):
                        r = None
                        for n, (pt, _, kbuf) in enumerate(pts):
                            r = e.matmul(po[0:64, :], lhsT=V[:, kbuf, r0:r0 + 64], rhs=pt[:, :], start=(n == 0), stop=(n == len(pts) - 1))
                        return r
                    pe(fo, [k for _, k, _ in pts] + [vk], [pok])
                    dve((lambda po, rd, kvh, c: lambda e: e.tensor_tensor(out=AO[:, kvh * 8:(kvh + 1) * 8, c * 64:(c + 1) * 64], in0=po[0:64, :].rearrange("p (h q) -> p h q", h=8), in1=rd[:, :].rearrange("p (h q) -> p h q", h=8), op=ALU.mult))(po, rd, kvh, c), [pok, rdk], ["AO"])
            wsl = [wload(wo_d[l, pc], nparts=64) for pc in range(4)]
            for blk in range(nb if "a_o" not in _SKIP else 0):
                for half in range(2):
                    p, pk = ps_next()

                    def f(e, p=p, blk=blk, half=half):
                        for h in range(16):
                            sv = wsl[h // 4][0][:].rearrange("p (k c) -> p k c", k=4)
                            e.matmul(p[0:bl, :], lhsT=AO[:, h, blk * bl:(blk + 1) * bl], rhs=sv[0:64, h % 4, half * 512:(half + 1) * 512], start=(h == 0), stop=False)
                        return e.matmul(p[0:bl, :], lhsT=e0[0:64, 0:bl], rhs=borow[0:64, half * 512:(half + 1) * 512], start=False, stop=True)
                    pe(f, ["AO", "borow", "e0"] + [k for _, k in wsl], [pk])
                    residual_evac(p, pk, blk, bl, half * 512, 512)
            if not is_sample:
                act(lambda e: e.activation(out=KT[:, 0:128], in_=KT[:, ntok:ntok + 128], func=AF.Copy), [kk], [kk])
                act(lambda e: e.activation(out=V[:, 0:2, :], in_=V[:, nch:nch + 2, :], func=AF.Copy), [vk], [vk])

        ctc = [0]

        def ssm(s, nb, bl, is_sample, is_last_prompt):
            ntok = nb * bl
            sbl = ntok
            nsb = 1
            bi = 0 if bl == 128 else 1
            c0s = c0[s]
            ck_ = ("c0", s)
            S.cur_tag = "ssm"
            S.barrier()
            areset()
            u_bf = abf(KC * TT, shape=(KC, TT))
            zT = abf(KC * TT, shape=(KC, TT))
            Gs2 = [[abf(TT) for _ in range(4)] for _ in range(2)]
            ctab = [abf(2560) for _ in range(2)]
            cbc = [abf(2048) for _ in range(2)]
            bglurow = abf(2048)
            bub2 = [[abf(TT) for _ in range(2)] for _ in range(2)]
            tq2 = [[abf(TT) for _ in range(4)] for _ in range(2)]
            wbf2 = [[abf(TT) for _ in range(2)] for _ in range(2)]
            dve(lambda e: e.memset(bglurow[:, :], 0.0), [], ["bglurow"])
            wsc2 = [[af(TT) for _ in range(2)] for _ in range(2)]
            ytmp = af(TT)
            sgt = af(512)
            mtmp = af(512)
            dma_in(binc[:], bin_d[s], ["binc"])
            dma_in(ddc[:], dd_d[s], ["ddc"])
            for hh_ in range(2):
                dma_in(rowf[:, 0:1024], bglu_d[s][:, hh_ * 1024:(hh_ + 1) * 1024], ["rowf"])
                dve((lambda hh_: lambda e: e.tensor_copy(out=bglurow[0:1, hh_ * 1024:(hh_ + 1) * 1024], in_=rowf[:, 0:1024]))(hh_), ["rowf"], ["bglurow"])
            if is_sample:
                dma_in(sm[0][:], sre_d[s], ["sm0"])
                dma_in(sm[1][:], sim_d[s], ["sm1"])
                ab = abar[s]
                dve(lambda e: e.tensor_tensor(out=sm[2][:], in0=sm[0][:], in1=ab[:, :, 0], op=ALU.mult), ["sm0", ("abar", s)], ["sm2"])
                dve(lambda e: e.tensor_tensor(out=sm[3][:], in0=sm[1][:], in1=ab[:, :, 1], op=ALU.mult), ["sm1", ("abar", s)], ["sm3"])
                dve(lambda e: e.tensor_tensor(out=c0s[:, :, 0], in0=sm[2][:], in1=sm[3][:], op=ALU.subtract), ["sm2", "sm3"], [ck_])
                dve(lambda e: e.tensor_tensor(out=sm[2][:], in0=sm[1][:], in1=ab[:, :, 0], op=ALU.mult), ["sm1", ("abar", s)], ["sm2"])
                dve(lambda e: e.tensor_tensor(out=sm[3][:], in0=sm[0][:], in1=ab[:, :, 1], op=ALU.mult), ["sm0", ("abar", s)], ["sm3"])
                dve(lambda e: e.tensor_tensor(out=c0s[:, :, 1], in0=sm[2][:], in1=sm[3][:], op=ALU.add), ["sm2", "sm3"], [ck_])
            xkeys = [("xT", k) for k in range(KC)]
            wsl = [wload(win_d[s, pc]) for pc in range(2)]
            for mt in range(KC):
                slot, sk = wsl[mt // 4]
                sv = slot[:].rearrange("p (k c) -> p k c", k=KC)
                c = (mt % 4) * 128
                p, pk = ps_next()

                def f(e, p=p, sv=sv, c=c):
                    r = None
                    for kc in range(KC):
                        r = e.matmul(p[:, 0:ntok], lhsT=sv[:, kc, c:c + 128], rhs=xT[:, kc, 0:ntok], start=(kc == 0), stop=(kc == KC - 1))
                    return r
                pe(f, xkeys + [sk], [pk])
                act(lambda e: e.activation(out=u_bf[:, mt, 0:ntok], in_=p[:, 0:ntok], func=AF.Identity, bias=binc[:, mt:mt + 1], scale=1.0), [pk, "binc"], [("u", mt)])
            want_state = (is_sample or is_last_prompt) and "s_state" not in _SKIP
            for kc in range(KC if "s_core" not in _SKIP else 0):
                ci = ctc[0] % 2
                ctc[0] += 1
                bc_, bk = cbc[ci], ("cbc", ci)
                S.op("pool", lambda e: e.dma_start(out=bc_[:, 0:1024], in_=bm_d[s, kc]), writes=[bk], dma=True, arena=True)
                S.op("pool", lambda e: e.dma_start(out=bc_[:, 1024:2048], in_=cm_d[s, kc]), writes=[bk], dma=True, arena=True)
                py, pyk = psb[6 + kc % 2], ("ps", 6 + kc % 2)
                for q4 in range(4):
                    q = 4 * kc + q4
                    par = q % 2
                    tab, tk = ctab[par], ("ctab", par)
                    S.op("sp", lambda e: e.dma_start(out=tab[:], in_=tab_d[s, q]), reads=[("tabd", s, q)], writes=[tk], dma=True, arena=True)
                    Gs, bub, tq, wbf, wsc = Gs2[par], bub2[par], tq2[par], wbf2[par], wsc2[par]
                    pr, prk = ps_next()
                    pi_, pik = ps_next()
                    pe(lambda e: (e.matmul(pr[:, 0:ntok], lhsT=bc_[:, q4 * 128:(q4 + 1) * 128], rhs=u_bf[:, kc, 0:ntok], start=True, stop=True), e.matmul(pi_[:, 0:ntok], lhsT=bc_[:, 512 + q4 * 128:512 + (q4 + 1) * 128], rhs=u_bf[:, kc, 0:ntok], start=True, stop=True)), [bk, ("u", kc)], [prk, pik])
                    act(lambda e: e.activation(out=bub[0][:, 0:ntok], in_=pr[:, 0:ntok], func=AF.Copy), [prk], [("bub0", par)])
                    act(lambda e: e.activation(out=bub[1][:, 0:ntok], in_=pi_[:, 0:ntok], func=AF.Copy), [pik], [("bub1", par)])

                    def tb(t):
                        return tab[:, t * 512:t * 512 + sbl].unsqueeze(1)

                    def v3(t):
                        return t[:, 0:ntok].rearrange("p (a b) -> p a b", a=nsb)
                    dve(lambda e: e.tensor_tensor(out=v3(tq[0]), in0=v3(bub[0]), in1=tb(0), op=ALU.mult), [("bub0", par), tk], [("tq0", par)])
                    dve(lambda e: e.tensor_tensor(out=v3(tq[1]), in0=v3(bub[1]), in1=tb(1), op=ALU.mult), [("bub1", par), tk], [("tq1", par)])
                    dve(lambda e: e.tensor_tensor(out=v3(tq[2]), in0=v3(bub[1]), in1=tb(0), op=ALU.mult), [("bub1", par), tk], [("tq2", par)])
                    dve(lambda e: e.tensor_tensor(out=v3(tq[3]), in0=v3(bub[0]), in1=tb(1), op=ALU.mult), [("bub0", par), tk], [("tq3", par)])
                    for blk in range(nsb if "s_scan" not in _SKIP else 0):
                        sl = slice(blk * sbl, (blk + 1) * sbl)
                        dve(lambda e: e.tensor_tensor_scan(out=wsc[0][:, sl], data0=tq[0][:, sl], data1=tq[1][:, sl], initial=c0s[:, q, 0:1], op0=ALU.add, op1=ALU.subtract), [("tq0", par), ("tq1", par), ck_, ("c0q", s, q)], [("wsc0", par)])
                        dve(lambda e: e.tensor_tensor_scan(out=wsc[1][:, sl], data0=tq[2][:, sl], data1=tq[3][:, sl], initial=c0s[:, q, 1:2], op0=ALU.add, op1=ALU.add), [("tq2", par), ("tq3", par), ck_, ("c0q", s, q)], [("wsc1", par)])
                        ecol = (blk + 1) * sbl - 1
                        if want_state and blk == nsb - 1:
                            fo = 2 * 512 + sbl - 1
                            nfo = 3 * 512 + sbl - 1
                            dve(lambda e: e.tensor_copy(out=tiny[:, 4:5], in_=tab[:, fo:fo + 1]), [tk], ["tiny"])
                            dve(lambda e: e.tensor_copy(out=tiny[:, 5:6], in_=tab[:, nfo:nfo + 1]), [tk], ["tiny"])
                            dve(lambda e: e.tensor_tensor(out=tiny[:, 0:1], in0=wsc[1][:, ecol:ecol + 1], in1=tiny[:, 5:6], op=ALU.mult), [("wsc1", par), "tiny"], ["tiny"])
                            dve(lambda e: e.scalar_tensor_tensor(out=send[:, q:q + 1], in0=wsc[0][:, ecol:ecol + 1], scalar=tiny[:, 4:5], in1=tiny[:, 0:1], op0=ALU.mult, op1=ALU.add), [("wsc0", par), "tiny"], ["send"])
                            dve(lambda e: e.tensor_tensor(out=tiny[:, 1:2], in0=wsc[0][:, ecol:ecol + 1], in1=tiny[:, 5:6], op=ALU.mult), [("wsc0", par), "tiny"], ["tiny"])
                            dve(lambda e: e.scalar_tensor_tensor(out=send[:, 32 + q:33 + q], in0=wsc[1][:, ecol:ecol + 1], scalar=tiny[:, 4:5], in1=tiny[:, 1:2], op0=ALU.mult, op1=ALU.subtract), [("wsc1", par), "tiny"], ["send"])
                        if not is_sample:
                            ar = abl[s][:, q, 2 * bi:2 * bi + 1]
                            ai = abl[s][:, q, 2 * bi + 1:2 * bi + 2]
                            ceng = os.environ.get("KCARRY", "dve")
                            tp = tinyp[par]
                            tpk = ("tinyp", par)
                            cq = ("c0q", s, q)
                            S.op(ceng, lambda e: e.tensor_tensor(out=tp[:, 0:1], in0=wsc[1][:, ecol:ecol + 1], in1=ai, op=ALU.mult), reads=[("wsc1", par), ("abl", s)], writes=[tpk])
                            S.op(ceng, lambda e: e.tensor_tensor(out=tp[:, 1:2], in0=wsc[0][:, ecol:ecol + 1], in1=ai, op=ALU.mult), reads=[("wsc0", par), ("abl", s)], writes=[tpk])
                            S.op(ceng, lambda e: e.tensor_scalar(out=c0s[:, q, 0:1], in0=wsc[0][:, ecol:ecol + 1], scalar1=ar, scalar2=tp[:, 0:1], op0=ALU.mult, op1=ALU.subtract), reads=[("wsc0", par), tpk, ("abl", s)], writes=[cq])
                            S.op(ceng, lambda e: e.tensor_scalar(out=c0s[:, q, 1:2], in0=wsc[1][:, ecol:ecol + 1], scalar1=ar, scalar2=tp[:, 1:2], op0=ALU.mult, op1=ALU.add), reads=[("wsc1", par), tpk, ("abl", s)], writes=[cq])
                    act(lambda e: e.activation(out=wbf[0][:, 0:ntok], in_=wsc[0][:, 0:ntok], func=AF.Copy), [("wsc0", par)], [("wbf0", par)])
                    act(lambda e: e.activation(out=wbf[1][:, 0:ntok], in_=wsc[1][:, 0:ntok], func=AF.Copy), [("wsc1", par)], [("wbf1", par)])
                    geng = os.environ.get("KGENG", "dve")
                    ngp = int(os.environ.get("KGPOOL", "0"))
                    for gi, (wi, ti) in enumerate(((0, 2), (1, 3), (1, 4), (0, 3))):
                        en = "pool" if gi < ngp else "dve"
                        S.op(en, lambda e: e.tensor_tensor(out=v3(Gs[gi]), in0=v3(wbf[wi]), in1=tb(ti), op=ALU.mult), reads=[("wbf%d" % wi, par), tk], writes=[("G%d" % gi, par)])

                    def fc(e):
                        cre = bc_[:, 1024 + q4 * 128:1024 + (q4 + 1) * 128]
                        cim = bc_[:, 1536 + q4 * 128:1536 + (q4 + 1) * 128]
                        e.matmul(py[:, 0:ntok], lhsT=cre, rhs=Gs[0][:, 0:ntok], start=(q4 == 0), stop=False)
                        e.matmul(py[:, 0:ntok], lhsT=cre, rhs=Gs[1][:, 0:ntok], start=False, stop=False)
                        e.matmul(py[:, 0:ntok], lhsT=cim, rhs=Gs[2][:, 0:ntok], start=False, stop=False)
                        return e.matmul(py[:, 0:ntok], lhsT=cim, rhs=Gs[3][:, 0:ntok], start=False, stop=(q4 == 3))
                    pe(fc, [bk, ("G0", par), ("G1", par), ("G2", par), ("G3", par)], [pyk])
                dve(lambda e: e.scalar_tensor_tensor(out=ytmp[:, 0:ntok], in0=u_bf[:, kc, 0:ntok], scalar=ddc[:, kc:kc + 1], in1=py[:, 0:ntok], op0=ALU.mult, op1=ALU.add), [pyk, ("u", kc), "ddc"], ["ytmp"])
                act(lambda e: e.activation(out=zT[:, kc, 0:ntok], in_=ytmp[:, 0:ntok], func=AF.Gelu_apprx_tanh), ["ytmp"], [("zT", kc)])
            if want_state:
                p, pk = ps_next()
                pe(lambda e: e.transpose(p[:, 0:128], send[:, :], ident[:]), ["send", "ident"], [pk])
                dve(lambda e: e.tensor_copy(out=sendT[:, :], in_=p[0:64, 0:128]), [pk], ["ktr"])
                if is_sample:
                    dma_out(nrs_d[s], sendT[0:32, :], ["ktr"])
                    dma_out(nis_d[s], sendT[32:64, :], ["ktr"])
                else:
                    dma_out(nrp_d[s], sendT[0:32, :], ["ktr"])
                    dma_out(nip_d[s], sendT[32:64, :], ["ktr"])
            S.cur_tag = "glu"
            zkeys = [("zT", k) for k in range(KC)]
            for pg in range(2 if "s_glu" not in _SKIP else 0):
                sa, sak = wload(wglu_d[s, 2 * pg])
                sb_, sbk = wload(wglu_d[s, 2 * pg + 1])
                sav = sa[:].rearrange("p (k c) -> p k c", k=KC)
                sbv = sb_[:].rearrange("p (k c) -> p k c", k=KC)
                for blk in range(nb):
                    pa, pak = ps_next()
                    pb, pbk = ps_next()

                    def f(e, pa=pa, pb=pb, blk=blk, pg=pg):
                        r = None
                        for p, sv, bo_ in ((pa, sav, 512 * pg), (pb, sbv, 1024 + 512 * pg)):
                            for kc in range(KC):
                                e.matmul(p[0:bl, :], lhsT=zT[:, kc, blk * bl:(blk + 1) * bl], rhs=sv[:, kc, :], start=(kc == 0), stop=False)
                            r = e.matmul(p[0:bl, :], lhsT=e0[:, 0:bl], rhs=bglurow[:, bo_:bo_ + 512], start=False, stop=True)
                        return r
                    pe(f, zkeys + [sak, sbk, "bglurow", "e0"], [pak, pbk])
                    act((lambda pb: lambda e: e.activation(out=sgt[0:bl, :], in_=pb[0:bl, :], func=AF.Sigmoid))(pb), [pbk], ["sgt"])
                    dve((lambda pa: lambda e: e.tensor_tensor(out=mtmp[0:bl, :], in0=pa[0:bl, :], in1=sgt[0:bl, :], op=ALU.mult))(pa), [pak, "sgt"], ["mtmp"])
                    dve((lambda blk, pg: lambda e: e.scalar_tensor_tensor(out=xres[0:bl, blk, 512 * pg:512 * pg + 512], in0=xres[0:bl, blk, 512 * pg:512 * pg + 512], scalar=ALPHA, in1=mtmp[0:bl, :], op0=ALU.mult, op1=ALU.add))(blk, pg), [("xr", blk), "mtmp"], [("xr", blk)])

        for s in range(n_ssm):
            dma_in(sm[0][:], ldt_d[s], ["sm0"])
            dma_in(sm[1][:], are_d[s], ["sm1"])
            dma_in(sm[2][:], aim_d[s], ["sm2"])
            act(lambda e: e.activation(out=sm[3][:], in_=sm[0][:], func=AF.Exp), ["sm0"], ["sm3"])
            dve(lambda e: e.tensor_tensor(out=sm[4][:], in0=sm[3][:], in1=sm[1][:], op=ALU.mult), ["sm3", "sm1"], ["sm4"])
            dve(lambda e: e.tensor_tensor(out=sm[5][:], in0=sm[3][:], in1=sm[2][:], op=ALU.mult), ["sm3", "sm2"], ["sm5"])
            dve(lambda e: e.tensor_scalar(out=sm[6][:], in0=sm[5][:], scalar1=1.0 / TWO_PI, scalar2=None, op0=ALU.mult), ["sm5"], ["sm6"])
            dve(lambda e: e.tensor_scalar(out=sm[7][:], in0=sm[6][:], scalar1=0.25, scalar2=None, op0=ALU.add), ["sm6"], ["sm7"])
            range_sin(sm[8][:], sm[6][:], smi[:], sm[11][:], ["sm6"], "sm8", "smi", "sm11")
            range_sin(sm[9][:], sm[7][:], smi[:], sm[11][:], ["sm7"], "sm9", "smi", "sm11")
            act(lambda e: e.activation(out=sm[10][:], in_=sm[4][:], func=AF.Exp), ["sm4"], ["sm10"])
            dve((lambda s: lambda e: e.tensor_tensor(out=abar[s][:, :, 0], in0=sm[10][:], in1=sm[9][:], op=ALU.mult))(s), ["sm10", "sm9"], [("abar", s)])
            dve((lambda s: lambda e: e.tensor_tensor(out=abar[s][:, :, 1], in0=sm[10][:], in1=sm[8][:], op=ALU.mult))(s), ["sm10", "sm8"], [("abar", s)])

        tiles = [(t * TT, 4, 128, False) for t in range(n_ptiles)]
        if with_sample and "sample" not in _SKIP:
            tiles.append((NTOKP, 1, 64, True))
        for ti, (pos0, nb, bl, is_sample) in enumerate(tiles):
            ntok = nb * bl
            first_prompt = (ti == 0)
            is_last_prompt = (not is_sample) and ti == n_ptiles - 1
            src = xs_d if is_sample else xp_d[pos0:pos0 + ntok, :]
            dma_in(xres[0:bl, 0:nb, :], src.rearrange("(b p) d -> p b d", p=bl), [("xr", b_) for b_ in range(4)])
            transpose_to_xT(nb, bl)
            if "rope" not in _SKIP:
                rope_tables(pos0, ntok)
            for i in range(depth):
                if i % 2 == 0:
                    if "attn" not in _SKIP:
                        attention(i // 2, nb, bl, first_prompt, is_sample, is_last_prompt)
                else:
                    if "ssm" not in _SKIP:
                        ssm(i // 2, nb, bl, is_sample, is_last_prompt)
                if "ln" not in _SKIP:
                    layer_norm(i, 0, nb, bl)
                transpose_to_xT(nb, bl)
                if "ffn" not in _SKIP:
                    ffn(i, nb, bl)
                if "ln" not in _SKIP:
                    layer_norm(i, 1, nb, bl)
                if i < depth - 1:
                    transpose_to_xT(nb, bl)
            dst = ys_d if is_sample else yp_d[pos0:pos0 + ntok, :]
            dma_out(dst.rearrange("(b p) d -> p b d", p=bl), xres[0:bl, 0:nb, :], [("xr", b_) for b_ in range(4)])
        S.emit_all()
    return nc


def _prep_shared(inp):
    f = np.float32
    sh = {}
    wq = inp["attn_w_qkv"]; bq = inp["attn_b_qkv"]
    wqkv = np.zeros((2, 5, 128, 8, 512), f)
    bqk = np.zeros((2, 128, 18), f)
    for l in range(2):
        cols = []
        for i in range(9):
            if i < 8:
                c = np.concatenate([np.arange(i * 64, i * 64 + 64), np.arange((8 + i) * 64, (8 + i) * 64 + 64)])
            else:
                c = np.arange(1024, 1152)
            cp = c.reshape(2, 2, 32)[:, ::-1, :].reshape(128)
            cols.append(c); cols.append(cp)
        cols.append(np.arange(1152, 1280))
        for mt, c in enumerate(cols):
            w = wq[l][:, c].reshape(8, 128, 128).transpose(1, 0, 2)
            wqkv[l, mt // 4, :, :, (mt % 4) * 128:(mt % 4) * 128 + 128] = w
            if mt < 18:
                bqk[l, :, mt] = bq[l][c]
    sh["wqkv"] = wqkv.reshape(2, 5, 128, 4096)
    sh["bqk"] = bqk
    sh["bv"] = np.ascontiguousarray(bq[:, 1152:1280].reshape(2, 1, 128))
    sh["snk"] = np.ascontiguousarray(np.repeat(inp["attn_sinks"].reshape(2, 16, 1), 64, axis=2).reshape(2, 1, 1024))
    sh["wo"] = np.ascontiguousarray(inp["attn_w_o"].reshape(2, 4, 4, 64, 1024).transpose(0, 1, 3, 2, 4).reshape(2, 4, 64, 4096))
    sh["bo"] = np.ascontiguousarray(inp["attn_b_o"].reshape(2, 1, 1024))
    sh["win"] = np.ascontiguousarray(inp["ssm_w_in"].reshape(2, 8, 128, 2, 512).transpose(0, 3, 2, 1, 4).reshape(2, 2, 128, 4096))
    sh["bin"] = np.ascontiguousarray(inp["ssm_b_in"].reshape(2, 8, 128).transpose(0, 2, 1))
    sh["dd"] = np.ascontiguousarray(inp["ssm_d"].reshape(2, 8, 128).transpose(0, 2, 1))
    sh["wglu"] = np.ascontiguousarray(inp["ssm_w_glu"].reshape(2, 8, 128, 2, 2, 512).transpose(0, 4, 3, 2, 1, 5).reshape(2, 4, 128, 4096))
    sh["bglu"] = np.ascontiguousarray(inp["ssm_b_glu"].reshape(2, 1, 2048))

    def smlay(a):
        return np.ascontiguousarray(a.reshape(2, 32, 2, 64).transpose(0, 2, 3, 1).reshape(2, 128, 32))
    sh["ldt"] = smlay(np.repeat(inp["ssm_log_dt"].reshape(2, 64, 1), 64, axis=2))
    sh["are"] = smlay(inp["ssm_a_re"])
    sh["aim"] = smlay(inp["ssm_a_im"])
    bm = np.zeros((2, 8, 128, 2, 4, 128), f)
    cm = np.zeros((2, 8, 128, 2, 4, 128), f)
    for part, (bsrc, csrc) in enumerate(((inp["ssm_b_re"], inp["ssm_c_re"]), (inp["ssm_b_im"], inp["ssm_c_im"]))):
        for q in range(32):
            kc, q4 = q // 4, q % 4
            for r in range(2):
                gi = 2 * q + r
                bm[:, kc, q4 * 32 + r * 16:q4 * 32 + r * 16 + 16, part, q4, r * 64:(r + 1) * 64] = bsrc[:, gi].transpose(0, 2, 1)
                cm[:, kc, r * 64:(r + 1) * 64, part, q4, q4 * 32 + r * 16:q4 * 32 + r * 16 + 16] = csrc[:, gi].transpose(0, 2, 1)
    sh["bm"] = bm.reshape(2, 8, 128, 1024)
    sh["cm"] = cm.reshape(2, 8, 128, 1024)
    wu = inp["ffn_w_up"]
    wup = np.zeros((4, 11, 128, 8, 512), f)
    for j in range(NJ):
        for which in range(2):
            c0_ = which * DFF + j * 128
            w = wu[:, :, c0_:c0_ + 128].reshape(4, 8, 128, 128).transpose(0, 2, 1, 3)
            mt = (j % 2) * 2 + which
            wup[:, j // 2, :, :, mt * 128:(mt + 1) * 128] = w
    sh["wup"] = wup.reshape(4, 11, 128, 4096)
    wd = np.zeros((4, 24, 128, 1024), f)
    wd[:, :NJ] = inp["ffn_w_down"].reshape(4, NJ, 128, 1024)
    sh["wdn"] = np.ascontiguousarray(wd.reshape(4, 6, 4, 128, 1024).transpose(0, 1, 3, 2, 4).reshape(4, 6, 128, 4096))
    sh["lng"] = np.ascontiguousarray(np.concatenate([inp["ln_gain"], inp["ln_bias"]], axis=-1).reshape(4, 2, 1, 2048))
    sh["ident"] = np.eye(128, dtype=f)
    sh["jgrid"] = np.ascontiguousarray(np.tile(np.arange(512, dtype=f), (128, 1)))
    invf = (10000.0 ** (-(np.arange(128) % 32) / 32.0)).astype(np.float64)
    sign = np.where((np.arange(128) % 64) < 32, -1.0, 1.0)
    sh["ropec"] = np.stack([invf / TWO_PI, TWO_PI * sign], axis=1).astype(f)
    return sh


_NC_CACHE = {}


def kernel(**inputs):
    inp = {k: np.asarray(v) for k, v in inputs.items()}
    n_ptiles = inp["x_prompt"].shape[1] // TT
    sh = _prep_shared(inp)
    sh["pos"] = np.arange(n_ptiles * TT + 64, dtype=np.float32).reshape(1, -1)
    sh["pos"][0, n_ptiles * TT:] = 4096.0 + np.arange(64)
    key = n_ptiles
    if key not in _NC_CACHE:
        _NC_CACHE[key] = build(n_ptiles=n_ptiles)
    nc = _NC_CACHE[key]

    def smst(a):
        return np.ascontiguousarray(a.reshape(32, 2, 64).transpose(1, 2, 0).reshape(128, 32))
    in_maps = []
    for b in range(8):
        m = dict(sh)
        m["xp"] = np.ascontiguousarray(inp["x_prompt"][b])
        m["xs"] = np.ascontiguousarray(inp["x_sample"][b])
        m["ck"] = np.ascontiguousarray(inp["cache_k"][:, b].reshape(2, 128, 128))
        m["cv"] = np.ascontiguousarray(inp["cache_v"][:, b].reshape(2, 128, 128))
        m["sre"] = np.stack([smst(inp["state_ssm_re"][s, b]) for s in range(2)])
        m["sim"] = np.stack([smst(inp["state_ssm_im"][s, b]) for s in range(2)])
        in_maps.append(m)
    ncores = int(os.environ.get("KCORES", "8"))
    res = run_bass_kernel_spmd(nc, in_maps[:ncores], core_ids=list(range(ncores)))
    R = list(res.results)
    while len(R) < 8:
        R.append(R[0])

    def gather(name, shape):
        return np.stack([np.asarray(R[b][name]).reshape(shape) for b in range(8)], axis=0)
    yp = gather("yp", (n_ptiles * TT, D))
    ys = gather("ys", (64, D))

    def kv(name):
        return np.ascontiguousarray(gather(name, (2, 128, 2, 64)).transpose(1, 0, 2, 3, 4))

    def stt(name):
        return np.ascontiguousarray(gather(name, (2, 64, 64)).transpose(1, 0, 2, 3))
    return (yp.astype(np.float32), ys.astype(np.float32), kv("nkp"), kv("nvp"), stt("nrp"), stt("nip"),
            kv("nks"), kv("nvs"), stt("nrs"), stt("nis"))
```
